# Optimizing a Trainium2 kernel written in Bass

```python
import math
import jax, jax.numpy as jnp
from jax import lax
import numpy as np

D_MODEL = 1024
BATCH = 16
SEQ = 2048
DEPTH = 2

NORM_EPS = 1e-6
ROPE_THETA = 500000.0
HEAD_DIM = 64
ROT_DIM = HEAD_DIM // 4
NEG_INF = -1e30
Q_BLOCK = 128
RNN_WIDTH = D_MODEL // 2
RNN_BLOCKS = 8
RNN_BLOCK = RNN_WIDTH // RNN_BLOCKS
CONV_WIDTH = 4
RGLRU_C = 8.0
DIFF_HEADS = (D_MODEL // 2) // (2 * HEAD_DIM)
DIFF_QK_WIDTH = DIFF_HEADS * 2 * HEAD_DIM
DIFF_V_DIM = 2 * HEAD_DIM
DIFF_WIDTH = DIFF_HEADS * DIFF_V_DIM
AB_IN_WIDTH = 2 * RNN_WIDTH + 2 * DIFF_QK_WIDTH + DIFF_WIDTH
DIL_HEADS = D_MODEL // HEAD_DIM
DIL_PATTERNS = ((128, 1), (512, 4), (2048, 16))
D_FF = -(-8 * D_MODEL // (3 * 256)) * 256
N_EVEN = (DEPTH + 1) // 2
N_ODD = DEPTH // 2

kernel_name = "hawk_diffattn_dilated_hybrid"


def rms_norm(x, g):
    xf = x.astype(jnp.float32)
    y = xf * lax.rsqrt(jnp.mean(xf * xf, axis=-1, keepdims=True) + NORM_EPS)
    return (y * g.astype(jnp.float32)).astype(x.dtype)


def rope_tables(seq_len):
    pos = jnp.arange(seq_len, dtype=jnp.float32)
    inv_freq = 1.0 / (ROPE_THETA ** (jnp.arange(0, ROT_DIM, 2, dtype=jnp.float32) / ROT_DIM))
    ang = pos[:, None] * inv_freq[None, :]
    return jnp.cos(ang), jnp.sin(ang)


def apply_partial_rope(t, cos, sin):
    shape = (cos.shape[0],) + (1,) * (t.ndim - 3) + (cos.shape[1],)
    c = cos.reshape(shape)
    s = sin.reshape(shape)
    tf = t.astype(jnp.float32)
    half = ROT_DIM // 2
    x1 = tf[..., :half]
    x2 = tf[..., half:ROT_DIM]
    out = jnp.concatenate([x1 * c - x2 * s, x2 * c + x1 * s, tf[..., ROT_DIM:]], axis=-1)
    return out.astype(t.dtype)


def rglru_block(xr, gate, conv_w, conv_b, wa, ba, wx, bx, lru_lambda):
    B, S, W = xr.shape
    xp = jnp.pad(xr, ((0, 0), (CONV_WIDTH - 1, 0), (0, 0)))
    u = conv_b + sum(xp[:, j:j + S, :] * conv_w[j] for j in range(CONV_WIDTH))
    ub = u.reshape(B, S, RNN_BLOCKS, RNN_BLOCK)
    r = jax.nn.sigmoid((jnp.einsum('bsgi,gij->bsgj', ub, wa).reshape(B, S, W) + ba).astype(jnp.float32))
    i = jax.nn.sigmoid((jnp.einsum('bsgi,gij->bsgj', ub, wx).reshape(B, S, W) + bx).astype(jnp.float32))
    log_a = -RGLRU_C * r * jax.nn.softplus(-lru_lambda.astype(jnp.float32))
    a = jnp.exp(log_a)
    b = jnp.sqrt(-jnp.expm1(2.0 * log_a)) * (i * u.astype(jnp.float32))

    def combine(left, right):
        a1, b1 = left
        a2, b2 = right
        return a1 * a2, a2 * b1 + b2

    _, h = lax.associative_scan(combine, (a, b), axis=1)
    return (h * jax.nn.gelu(gate.astype(jnp.float32))).astype(xr.dtype)


def diff_attention(q, k, v, lam, lambda_init, subln_g):
    B, S, H, _, Dh = q.shape
    nb = S // Q_BLOCK
    qb = q.reshape(B, nb, Q_BLOCK, H, 2, Dh).transpose(1, 0, 2, 3, 4, 5)
    kf = k.astype(jnp.float32)
    vf = v.astype(jnp.float32)
    k_pos = jnp.arange(S)

    def one_block(args):
        qblk, bi = args
        s = jnp.einsum('bqhcd,bkhcd->bhcqk', qblk.astype(jnp.float32), kf)
        q_pos = bi * Q_BLOCK + jnp.arange(Q_BLOCK)
        mask = k_pos[None, :] <= q_pos[:, None]
        p = jax.nn.softmax(jnp.where(mask, s, NEG_INF), axis=-1)
        w = p[:, :, 0] - lam * p[:, :, 1]
        return jnp.einsum('bhqk,bkhd->bqhd', w, vf)

    o = lax.map(one_block, (qb, jnp.arange(nb)))
    o = o.transpose(1, 0, 2, 3, 4).reshape(B, S, H, 2 * Dh)
    o = rms_norm(o, subln_g) * (1.0 - lambda_init)
    return o.reshape(B, S, H * 2 * Dh).astype(v.dtype)


def dilated_window_attn(q, k, v, window, dilation):
    B, S, H, Dh = q.shape
    blk = window // dilation
    unit = blk * dilation
    s_pad = -(-S // unit) * unit
    lc = s_pad // dilation
    nb = lc // blk

    def to_blocks(t):
        t = jnp.pad(t.astype(jnp.float32), ((0, 0), (0, s_pad - S), (0, 0), (0, 0)))
        t = t.reshape(B, lc, dilation, H, Dh).transpose(0, 2, 3, 1, 4)
        return t.reshape(B, dilation, H, nb, blk, Dh)

    def with_prev(t):
        prev = jnp.pad(t[:, :, :, :-1], ((0, 0), (0, 0), (0, 0), (1, 0), (0, 0), (0, 0)))
        return jnp.concatenate([prev, t], axis=4)

    qb = to_blocks(q)
    k2 = with_prev(to_blocks(k))
    v2 = with_prev(to_blocks(v))
    s = jnp.einsum('brhnqd,brhnkd->brhnqk', qb, k2)
    qi = jnp.arange(blk)[:, None] + blk
    ki = jnp.arange(2 * blk)[None, :]
    dist = qi - ki
    band = (dist >= 0) & (dist <= blk)
    has_prev = (jnp.arange(nb)[:, None, None] > 0) | (ki[None] >= blk)
    mask = band[None] & has_prev
    s = jnp.where(mask, s, NEG_INF)
    m = jnp.max(s, axis=-1, keepdims=True)
    e = jnp.exp(s - m)
    den = jnp.sum(e, axis=-1)
    o = jnp.einsum('brhnqk,brhnkd->brhnqd', e, v2) / den[..., None]
    lse = m[..., 0] + jnp.log(den)
    o = o.reshape(B, dilation, H, lc, Dh).transpose(0, 3, 1, 2, 4).reshape(B, s_pad, H, Dh)[:, :S]
    lse = lse.reshape(B, dilation, H, lc).transpose(0, 3, 1, 2).reshape(B, s_pad, H)[:, :S]
    return o, lse


def hawk_diff_mixer(x, cos, sin, layer_idx, norm_g, w_in, conv_w, conv_b, wa, ba, wx, bx,
                    lru_lambda, q_norm_g, k_norm_g, lq1, lk1, lq2, lk2, subln_g, w_out):
    B, S, _ = x.shape
    h = rms_norm(x, norm_g)
    proj = h @ w_in
    cuts = np.cumsum([RNN_WIDTH, RNN_WIDTH, DIFF_QK_WIDTH, DIFF_QK_WIDTH]).tolist()
    xr, gate, q, k, v = jnp.split(proj, cuts, axis=-1)
    y_rnn = rglru_block(xr, gate, conv_w, conv_b, wa, ba, wx, bx, lru_lambda)
    q = q.reshape(B, S, DIFF_HEADS, 2, HEAD_DIM)
    k = k.reshape(B, S, DIFF_HEADS, 2, HEAD_DIM)
    v = v.reshape(B, S, DIFF_HEADS, DIFF_V_DIM)
    q = apply_partial_rope(rms_norm(q, q_norm_g), cos, sin) * (HEAD_DIM ** -0.5)
    k = apply_partial_rope(rms_norm(k, k_norm_g), cos, sin)
    lambda_init = 0.8 - 0.6 * math.exp(-0.3 * layer_idx)
    f32 = jnp.float32
    lam = (jnp.exp(jnp.sum(lq1.astype(f32) * lk1.astype(f32)))
           - jnp.exp(jnp.sum(lq2.astype(f32) * lk2.astype(f32))) + lambda_init)
    y_diff = diff_attention(q, k, v, lam, lambda_init, subln_g)
    return jnp.concatenate([y_rnn, y_diff], axis=-1) @ w_out


def dilated_mixer(x, cos, sin, norm_g, w_qkv, q_norm_g, k_norm_g, w_out):
    B, S, _ = x.shape
    h = rms_norm(x, norm_g)
    q, k, v = jnp.split(h @ w_qkv, 3, axis=-1)
    q = q.reshape(B, S, DIL_HEADS, HEAD_DIM)
    k = k.reshape(B, S, DIL_HEADS, HEAD_DIM)
    v = v.reshape(B, S, DIL_HEADS, HEAD_DIM)
    q = apply_partial_rope(rms_norm(q, q_norm_g), cos, sin) * (HEAD_DIM ** -0.5)
    k = apply_partial_rope(rms_norm(k, k_norm_g), cos, sin)
    outs = []
    lses = []
    for window, dilation in DIL_PATTERNS:
        o, lse = dilated_window_attn(q, k, v, window, dilation)
        outs.append(o)
        lses.append(lse)
    alpha = jax.nn.softmax(jnp.stack(lses), axis=0)
    o = jnp.einsum('gbsh,gbshd->bshd', alpha, jnp.stack(outs))
    return o.reshape(B, S, D_MODEL).astype(x.dtype) @ w_out


def swiglu(x, norm_g, w_gate, w_up, w_down):
    h = rms_norm(x, norm_g)
    return (jax.nn.silu(h @ w_gate) * (h @ w_up)) @ w_down


def setup_inputs(seed: int = 0) -> dict:
    key = jax.random.key(seed)
    ks = jax.random.split(key, 32)
    f32 = jnp.float32

    def normal(k, shape, scale):
        return jax.random.normal(k, shape, f32) * scale

    def gain(k, shape):
        return 1.0 + 0.02 * jax.random.normal(k, shape, f32)

    u = jax.random.uniform(ks[9], (N_EVEN, RNN_WIDTH), f32, minval=0.9, maxval=0.999)
    a0 = u ** (1.0 / RGLRU_C)
    lru_lambda = jnp.log(a0) - jnp.log1p(-a0)
    return {
        "x": jax.random.normal(ks[0], (BATCH, SEQ, D_MODEL), f32),
        "ab_norm_g": gain(ks[1], (N_EVEN, D_MODEL)),
        "ab_w_in": normal(ks[2], (N_EVEN, D_MODEL, AB_IN_WIDTH), D_MODEL ** -0.5),
        "ab_conv_w": normal(ks[3], (N_EVEN, CONV_WIDTH, RNN_WIDTH), CONV_WIDTH ** -0.5),
        "ab_conv_b": normal(ks[4], (N_EVEN, RNN_WIDTH), 0.01),
        "ab_wa": normal(ks[5], (N_EVEN, RNN_BLOCKS, RNN_BLOCK, RNN_BLOCK), RNN_BLOCK ** -0.5),
        "ab_ba": normal(ks[6], (N_EVEN, RNN_WIDTH), 0.01),
        "ab_wx": normal(ks[7], (N_EVEN, RNN_BLOCKS, RNN_BLOCK, RNN_BLOCK), RNN_BLOCK ** -0.5),
        "ab_bx": normal(ks[8], (N_EVEN, RNN_WIDTH), 0.01),
        "ab_lru_lambda": lru_lambda,
        "ab_q_norm_g": gain(ks[10], (N_EVEN, HEAD_DIM)),
        "ab_k_norm_g": gain(ks[11], (N_EVEN, HEAD_DIM)),
        "ab_lambda_q1": normal(ks[12], (N_EVEN, HEAD_DIM), 0.1),
        "ab_lambda_k1": normal(ks[13], (N_EVEN, HEAD_DIM), 0.1),
        "ab_lambda_q2": normal(ks[14], (N_EVEN, HEAD_DIM), 0.1),
        "ab_lambda_k2": normal(ks[15], (N_EVEN, HEAD_DIM), 0.1),
        "ab_subln_g": gain(ks[16], (N_EVEN, DIFF_V_DIM)),
        "ab_w_out": normal(ks[17], (N_EVEN, RNN_WIDTH + DIFF_WIDTH, D_MODEL), (RNN_WIDTH + DIFF_WIDTH) ** -0.5),
        "c_norm_g": gain(ks[18], (N_ODD, D_MODEL)),
        "c_w_qkv": normal(ks[19], (N_ODD, D_MODEL, 3 * D_MODEL), D_MODEL ** -0.5),
        "c_q_norm_g": gain(ks[20], (N_ODD, HEAD_DIM)),
        "c_k_norm_g": gain(ks[21], (N_ODD, HEAD_DIM)),
        "c_w_out": normal(ks[22], (N_ODD, D_MODEL, D_MODEL), D_MODEL ** -0.5),
        "ffn_norm_g": gain(ks[23], (DEPTH, D_MODEL)),
        "ffn_w_gate": normal(ks[24], (DEPTH, D_MODEL, D_FF), D_MODEL ** -0.5),
        "ffn_w_up": normal(ks[25], (DEPTH, D_MODEL, D_FF), D_MODEL ** -0.5),
        "ffn_w_down": normal(ks[26], (DEPTH, D_FF, D_MODEL), D_FF ** -0.5),
    }


def reference(x, ab_norm_g, ab_w_in, ab_conv_w, ab_conv_b, ab_wa, ab_ba, ab_wx, ab_bx,
              ab_lru_lambda, ab_q_norm_g, ab_k_norm_g, ab_lambda_q1, ab_lambda_k1,
              ab_lambda_q2, ab_lambda_k2, ab_subln_g, ab_w_out, c_norm_g, c_w_qkv,
              c_q_norm_g, c_k_norm_g, c_w_out, ffn_norm_g, ffn_w_gate, ffn_w_up, ffn_w_down):
    cos, sin = rope_tables(x.shape[1])
    for layer in range(DEPTH):
        j = layer // 2
        if layer % 2 == 0:
            x = x + hawk_diff_mixer(x, cos, sin, layer, ab_norm_g[j], ab_w_in[j], ab_conv_w[j],
                                    ab_conv_b[j], ab_wa[j], ab_ba[j], ab_wx[j], ab_bx[j],
                                    ab_lru_lambda[j], ab_q_norm_g[j], ab_k_norm_g[j],
                                    ab_lambda_q1[j], ab_lambda_k1[j], ab_lambda_q2[j],
                                    ab_lambda_k2[j], ab_subln_g[j], ab_w_out[j])
        else:
            x = x + dilated_mixer(x, cos, sin, c_norm_g[j], c_w_qkv[j], c_q_norm_g[j],
                                  c_k_norm_g[j], c_w_out[j])
        x = x + swiglu(x, ffn_norm_g[layer], ffn_w_gate[layer], ffn_w_up[layer], ffn_w_down[layer])
    return x
```

```python
import math
import contextlib
import numpy as np
import concourse.bass as bass
import concourse.mybir as mybir
from concourse.bass_utils import run_bass_kernel_spmd

F32 = mybir.dt.float32
BF16 = mybir.dt.bfloat16
AF = mybir.ActivationFunctionType
ALU = mybir.AluOpType
AX = mybir.AxisListType

NCORES = 8
S = 2048
D = 1024
TT = 512
NTT = S // TT
NQB = S // 128
DFF = 2816
NHB = DFF // 128
EPS = 1e-6
EPOCH = 1500
GR = 512


class Op:
    pass


class _Stop(Exception):
    pass


class Prog:
    ENGS = ("pe", "act", "dve", "pool", "sp")

    def __init__(self, nc):
        self.nc = nc
        self.ops = {e: [] for e in self.ENGS}
        self.lanes = {}
        self.last_w = {}
        self.readers = {}

    def _tok(self, o):
        return ("l", o.lane, o.lidx) if o.is_dma else ("e", o.eng, o.idx)

    def op(self, eng, fn, reads=(), writes=(), lane=None):
        o = Op()
        o.eng, o.fn, o.lane = eng, fn, lane
        o.is_dma = lane is not None
        o.deps, o.raw, o.signal, o.sigidx = set(), set(), False, None
        o.idx = len(self.ops[eng])
        self.ops[eng].append(o)
        if o.is_dma:
            L = self.lanes.setdefault(lane, [])
            o.lidx = len(L)
            if L:
                o.deps.add(self._tok(L[-1]))
            L.append(o)
        tok = self._tok(o)
        for r in reads:
            lw = self.last_w.get(r)
            if lw is not None:
                o.deps.add(lw)
                o.raw.add(lw)
        for w in writes:
            lw = self.last_w.get(w)
            if lw is not None:
                o.deps.add(lw)
            rl = self.readers.get(w)
            if rl:
                o.deps.update(rl)
        for r in reads:
            self.readers.setdefault(r, []).append(tok)
        for w in writes:
            self.last_w[w] = tok
            self.readers[w] = []
        o.deps.discard(tok)
        return o

    def emit(self):
        nc = self.nc
        known = {e: {} for e in self.ENGS}
        waits = {}
        for e in self.ENGS:
            kn = known[e]
            for o in self.ops[e]:
                need = {}
                for (kind, key, idx) in o.deps:
                    if kind == "e" and key == e and not o.is_dma:
                        if e in ("pe", "sp"):
                            continue
                        if (kind, key, idx) not in o.raw or o.idx - idx > 3:
                            continue
                    k = (kind, key)
                    if kn.get(k, -1) >= idx:
                        continue
                    if need.get(k, -1) < idx:
                        need[k] = idx
                wl = []
                for k, idx in need.items():
                    kn[k] = idx
                    wl.append((k[0], k[1], idx))
                    if k[0] == "e":
                        self.ops[k[1]][idx].signal = True
                waits[o] = wl
        nsig, cum = {}, {}
        for e in self.ENGS:
            c, arr = 0, []
            for o in self.ops[e]:
                if o.signal and not o.is_dma:
                    o.sigidx = c
                    c += 1
                arr.append(c)
            nsig[e], cum[e] = c, arr
        sems = {}
        st = contextlib.ExitStack()
        for e in self.ENGS:
            for ep in range(max((nsig[e] + EPOCH - 1) // EPOCH, 1)):
                sems[("e", e, ep)] = st.enter_context(nc.semaphore(f"s_{e}_{ep}"))
        for ln in self.lanes:
            sems[("l", ln)] = st.enter_context(nc.semaphore(f"l_{ln}"))
        self.n_sems = len(sems)
        block = st.enter_context(nc.Block())
        engmap = {"pe": block.tensor, "act": block.scalar, "dve": block.vector,
                  "pool": block.gpsimd, "sp": block.sync}

        def make_body(e):
            def body(eng):
                for o in self.ops[e]:
                    for (kind, key, idx) in waits[o]:
                        if kind == "e":
                            c = cum[key][idx]
                            ep = (c - 1) // EPOCH
                            eng.wait_ge(sems[("e", key, ep)], c - ep * EPOCH)
                        else:
                            eng.wait_ge(sems[("l", key)], 16 * (idx + 1))
                    ins = o.fn(eng)
                    if o.is_dma:
                        ins.then_inc(sems[("l", o.lane)], 16)
                    elif o.signal:
                        ins.then_inc(sems[("e", e, o.sigidx // EPOCH)], 1)
            return body

        for e in self.ENGS:
            if self.ops[e]:
                engmap[e](make_body(e))
        st.close()


class Buf:
    def __init__(self, arena_bf, off, dtype, cols, name):
        self.off, self.dtype, self.cols, self.name = off, dtype, cols, name
        self.esz = 4 if dtype == F32 else 2
        v = arena_bf[:, off // 2: off // 2 + (cols * self.esz) // 2]
        self.ap = v.bitcast(F32) if dtype == F32 else v

    def res(self, lo=0, hi=None):
        hi = self.cols if hi is None else hi
        b0 = self.off + lo * self.esz
        b1 = self.off + hi * self.esz
        return [f"g{g}" for g in range(b0 // GR, (b1 - 1) // GR + 1)]


CST_COLS = 2 * S + 16 * 128 + 5 * 128
C_OFF, S_OFF, ML1_OFF = 0, S, 2 * S
MCUR_OFF = ML1_OFF + 16 * 128
ONES_OFF = MCUR_OFF + 128
BONES_OFF = ONES_OFF + 128
IDENT_OFF = BONES_OFF + 128
R_OFF = IDENT_OFF + 128


def make_consts():
    cst = np.zeros((128, CST_COLS), np.float32)
    pos = np.arange(S, dtype=np.float32)
    inv_freq = (1.0 / (np.float32(500000.0) ** (np.arange(0, 16, 2, dtype=np.float32) / np.float32(16)))).astype(np.float32)
    ang = pos[None, :] * inv_freq[:, None]
    cos, sin = np.cos(ang).astype(np.float32), np.sin(ang).astype(np.float32)
    C = np.ones((128, S), np.float32)
    Sg = np.zeros((128, S), np.float32)
    R = np.zeros((128, 128), np.float32)
    for g in range(2):
        for d in range(8):
            C[64 * g + d] = cos[d]
            C[64 * g + 8 + d] = cos[d]
            Sg[64 * g + d] = -sin[d]
            Sg[64 * g + 8 + d] = sin[d]
            R[64 * g + 8 + d, 64 * g + d] = 1.0
            R[64 * g + d, 64 * g + 8 + d] = 1.0
    cst[:, C_OFF:C_OFF + S] = C
    cst[:, S_OFF:S_OFF + S] = Sg
    k = np.arange(128)[:, None]
    q = np.arange(128)[None, :]
    for dl in range(16):
        dist = 128 * dl + q - k
        m = ((dist >= 0) & (dist <= 128)).astype(np.float32)
        m += ((dist >= 0) & (dist <= 512) & (dist % 4 == 0)).astype(np.float32)
        m += ((dist >= 0) & (dist % 16 == 0)).astype(np.float32)
        lm = np.where(m > 0, 8.0 * np.log(np.maximum(m, 1.0)), -1000.0).astype(np.float32)
        cst[:, ML1_OFF + 128 * dl: ML1_OFF + 128 * (dl + 1)] = lm
    cst[:, MCUR_OFF:MCUR_OFF + 128] = np.where(q >= k, 0.0, -1000.0).astype(np.float32)
    cst[:, ONES_OFF:ONES_OFF + 128] = 1.0
    bo = np.zeros((128, 128), np.float32)
    bo[:64, :64] = 1.0
    bo[64:, 64:] = 1.0
    cst[:, BONES_OFF:BONES_OFF + 128] = bo
    cst[:, IDENT_OFF:IDENT_OFF + 128] = np.eye(128, dtype=np.float32)
    cst[:, R_OFF:R_OFF + 128] = R
    return cst


PP_NORM = 0
PP_CONVW = 32
PP_CONVB = 48
PP_BA = 52
PP_BX = 56
PP_LAM = 60
PP_QKG = 64
PP_SUBG = 68
PP_COLS = 69
PF_COLS = 128 + 4 * 64


def relayout_kn(W):
    K, N = W.shape
    return np.ascontiguousarray(W.reshape(K // 128, 128, N // 128, 128).transpose(2, 1, 0, 3)).reshape(N // 128, 128, (K // 128) * 128)


def prep_shared(inp):
    f = lambda a: np.asarray(a, dtype=np.float32)
    sh = {}
    sh["w_in"] = relayout_kn(f(inp["ab_w_in"])[0])
    sh["w_out0"] = relayout_kn(f(inp["ab_w_out"])[0])
    sh["w_qkv"] = relayout_kn(f(inp["c_w_qkv"])[0])
    sh["w_out1"] = relayout_kn(f(inp["c_w_out"])[0])
    for l in range(2):
        sh[f"w_gate{l}"] = relayout_kn(f(inp["ffn_w_gate"])[l])
        sh[f"w_up{l}"] = relayout_kn(f(inp["ffn_w_up"])[l])
        sh[f"w_down{l}"] = np.ascontiguousarray(f(inp["ffn_w_down"])[l].reshape(NHB, 128, D))
    pp = np.zeros((128, PP_COLS), np.float32)
    for i, g in enumerate([f(inp["ab_norm_g"])[0], f(inp["ffn_norm_g"])[0], f(inp["c_norm_g"])[0], f(inp["ffn_norm_g"])[1]]):
        pp[:, PP_NORM + 8 * i: PP_NORM + 8 * i + 8] = g.reshape(8, 128).T
    cw = f(inp["ab_conv_w"])[0]
    for c4 in range(4):
        pp[:, PP_CONVW + 4 * c4: PP_CONVW + 4 * c4 + 4] = cw[:, 128 * c4:128 * c4 + 128].T
    pp[:, PP_CONVB:PP_CONVB + 4] = f(inp["ab_conv_b"])[0].reshape(4, 128).T
    pp[:, PP_BA:PP_BA + 4] = f(inp["ab_ba"])[0].reshape(4, 128).T
    pp[:, PP_BX:PP_BX + 4] = f(inp["ab_bx"])[0].reshape(4, 128).T
    pp[:, PP_LAM:PP_LAM + 4] = f(inp["ab_lru_lambda"])[0].reshape(4, 128).T
    for i, nm in enumerate(["ab_q_norm_g", "ab_k_norm_g", "c_q_norm_g", "c_k_norm_g"]):
        pp[:, PP_QKG + i] = np.tile(f(inp[nm])[0], 2)
    pp[:, PP_SUBG] = f(inp["ab_subln_g"])[0]
    sh["pp"] = pp
    pf = np.zeros((128, PF_COLS), np.float32)
    pf[:, 0:128] = f(inp["ab_subln_g"])[0][None, :]
    for i, nm in enumerate(["ab_lambda_q1", "ab_lambda_k1", "ab_lambda_q2", "ab_lambda_k2"]):
        pf[:, 128 + 64 * i: 128 + 64 * (i + 1)] = f(inp[nm])[0][None, :]
    sh["pf"] = pf
    for nm, key in (("wa_bd", "ab_wa"), ("wx_bd", "ab_wx")):
        w = f(inp[key])[0]
        bd = np.zeros((4, 128, 128), np.float32)
        for c4 in range(4):
            bd[c4, :64, :64] = w[2 * c4]
            bd[c4, 64:, 64:] = w[2 * c4 + 1]
        sh[nm] = bd
    sh["cst"] = make_consts()
    return sh


def build_program(nseq=2, dump=None, stop_after=None):
    dump = dump or {}
    nc = bass.Bass("TRN2", target_bir_lowering=False)
    P = Prog(nc)

    def din(name, shape):
        return nc.dram_tensor(name, list(shape), F32, kind="ExternalInput").ap()

    xT = din("xT", [nseq, D, S])
    oT = nc.dram_tensor("oT", [nseq, D, S], F32, kind="ExternalOutput").ap()
    w_in = din("w_in", [20, 128, 1024])
    w_out0 = din("w_out0", [8, 128, 1024])
    w_qkv = din("w_qkv", [24, 128, 1024])
    w_out1 = din("w_out1", [8, 128, 1024])
    w_gate = [din(f"w_gate{l}", [NHB, 128, 1024]) for l in range(2)]
    w_up = [din(f"w_up{l}", [NHB, 128, 1024]) for l in range(2)]
    w_down = [din(f"w_down{l}", [NHB, 128, 1024]) for l in range(2)]
    pp_d = din("pp", [128, PP_COLS])
    pf_d = din("pf", [128, PF_COLS])
    wa_d = din("wa_bd", [4, 128, 128])
    wx_d = din("wx_bd", [4, 128, 128])
    cst_d = din("cst", [128, CST_COLS])
    dump_out = {}
    for nm, shp in dump.items():
        dump_out[nm] = nc.dram_tensor("dbg_" + nm, list(shp), F32, kind="ExternalOutput").ap()

    ARENA_BYTES = 206 * 1024
    arena = nc.alloc_sbuf_tensor("arena", [128, ARENA_BYTES // 2], BF16).ap()
    cur = [0]

    def alloc(name, dtype, cols):
        esz = 4 if dtype == F32 else 2
        nb = ((cols * esz + GR - 1) // GR) * GR
        off = cur[0]
        cur[0] += nb
        assert cur[0] <= ARENA_BYTES, (name, cur[0])
        return Buf(arena, off, dtype, cols, name)

    X = alloc("X", F32, 8 * S)
    XN = alloc("XN", BF16, 8 * S)
    Y = alloc("Y", BF16, 8 * S)
    NW = 8
    WR = [alloc(f"W{i}", BF16, 1024) for i in range(NW)]
    CSTB = alloc("CST", BF16, CST_COLS)
    PPB = alloc("PP", F32, PP_COLS + 60)
    PFB = alloc("PF", F32, PF_COLS)
    BD = alloc("BD", BF16, 8 * 128)
    RG = alloc("RG", BF16, 4 * 128)
    scratch0 = cur[0]
    SCR_BYTES = ARENA_BYTES - scratch0

    def scr_alloc_reset():
        cur[0] = scratch0

    Xv = X.ap.rearrange("p (c t) -> p c t", c=8)
    XNv = XN.ap.rearrange("p (c t) -> p c t", c=8)
    Yv = Y.ap.rearrange("p (c t) -> p c t", c=8)
    cst = CSTB.ap
    Ctab = cst[:, C_OFF:C_OFF + S]
    Stab = cst[:, S_OFF:S_OFF + S]
    ones_bf = cst[:, ONES_OFF:ONES_OFF + 128]
    bones_bf = cst[:, BONES_OFF:BONES_OFF + 128]
    ident_bf = cst[:, IDENT_OFF:IDENT_OFF + 128]
    pp = PPB.ap
    pf = PFB.ap
    DV = PP_COLS
    C_EPS, C_NSP, C_NSP2, C_NLAM, C_TMP, C_SUBG = DV, DV + 1, DV + 5, DV + 9, DV + 10, DV + 12

    ps_all = nc.alloc_psum_tensor("ps", [128, 8, 512], F32).ap()

    def PS(b):
        return ps_all[:, b, :]

    def PSR(b):
        return [f"ps{b}"]

    def xres(c, tt):
        return X.res(c * S + tt * TT, c * S + (tt + 1) * TT)

    def xnres(c, tt):
        return XN.res(c * S + tt * TT, c * S + (tt + 1) * TT)

    def yres(c, lo, hi):
        return Y.res(c * S + lo, c * S + hi)

    for i in range(0, CST_COLS, 1024):
        j = min(i + 1024, CST_COLS)
        P.op("pool", lambda e, i=i, j=j: e.dma_start(out=cst[:, i:j], in_=cst_d[:, i:j]),
             writes=CSTB.res(i, j), lane="cst")
    P.op("sp", lambda e: e.dma_start(out=pp[:, 0:PP_COLS], in_=pp_d), writes=PPB.res(), lane="io0")
    P.op("sp", lambda e: e.dma_start(out=pf, in_=pf_d), writes=PFB.res(), lane="io1")
    for c4 in range(4):
        P.op("pool", lambda e, c4=c4: e.dma_start(out=BD.ap[:, c4 * 128:(c4 + 1) * 128], in_=wa_d[c4]),
             writes=BD.res(), lane="cst")
        P.op("pool", lambda e, c4=c4: e.dma_start(out=BD.ap[:, (4 + c4) * 128:(5 + c4) * 128], in_=wx_d[c4]),
             writes=BD.res(), lane="cst")
    P.op("dve", lambda e: e.memset(pp[:, C_EPS:C_EPS + 1], EPS), writes=PPB.res())
    P.op("act", lambda e: e.activation(out=pp[:, C_NSP:C_NSP + 4], in_=pp[:, PP_LAM:PP_LAM + 4], func=AF.Exp, scale=-1.0),
         reads=PPB.res(), writes=PPB.res())
    P.op("act", lambda e: e.activation(out=pp[:, C_NSP:C_NSP + 4], in_=pp[:, C_NSP:C_NSP + 4], func=AF.Ln, bias=1.0),
         reads=PPB.res(), writes=PPB.res())
    P.op("dve", lambda e: e.tensor_scalar(out=pp[:, C_NSP2:C_NSP2 + 4], in0=pp[:, C_NSP:C_NSP + 4], scalar1=-16.0, scalar2=None, op0=ALU.mult),
         reads=PPB.res(), writes=PPB.res())
    P.op("dve", lambda e: e.tensor_scalar(out=pp[:, C_NSP:C_NSP + 4], in0=pp[:, C_NSP:C_NSP + 4], scalar1=-8.0, scalar2=None, op0=ALU.mult),
         reads=PPB.res(), writes=PPB.res())
    LAMBDA_INIT0 = 0.8 - 0.6 * math.exp(-0.3 * 0)
    for i in range(2):
        a0 = 128 + 128 * i
        P.op("dve", lambda e, a0=a0: e.tensor_tensor(out=pf[:, a0:a0 + 64], in0=pf[:, a0:a0 + 64], in1=pf[:, a0 + 64:a0 + 128], op=ALU.mult),
             reads=PFB.res(), writes=PFB.res())
        P.op("dve", lambda e, a0=a0, i=i: e.tensor_reduce(out=pp[:, C_TMP + i:C_TMP + i + 1], in_=pf[:, a0:a0 + 64], axis=AX.X, op=ALU.add),
             reads=PFB.res(), writes=PPB.res())
    P.op("act", lambda e: e.activation(out=pp[:, C_TMP:C_TMP + 2], in_=pp[:, C_TMP:C_TMP + 2], func=AF.Exp),
         reads=PPB.res(), writes=PPB.res())
    P.op("dve", lambda e: e.tensor_tensor(out=pp[:, C_NLAM:C_NLAM + 1], in0=pp[:, C_TMP + 1:C_TMP + 2], in1=pp[:, C_TMP:C_TMP + 1], op=ALU.subtract),
         reads=PPB.res(), writes=PPB.res())
    P.op("dve", lambda e: e.tensor_scalar(out=pp[:, C_NLAM:C_NLAM + 1], in0=pp[:, C_NLAM:C_NLAM + 1], scalar1=-LAMBDA_INIT0, scalar2=None, op0=ALU.add),
         reads=PPB.res(), writes=PPB.res())
    P.op("dve", lambda e: e.tensor_scalar(out=pp[:, C_SUBG:C_SUBG + 1], in0=pp[:, PP_SUBG:PP_SUBG + 1], scalar1=1.0 - LAMBDA_INIT0, scalar2=None, op0=ALU.mult),
         reads=PPB.res(), writes=PPB.res())
    P.op("dve", lambda e: e.tensor_scalar(out=pf[:, 0:128], in0=pf[:, 0:128], scalar1=1.0 - LAMBDA_INIT0, scalar2=None, op0=ALU.mult),
         reads=PFB.res(), writes=PFB.res())
    for i in range(4):
        P.op("dve", lambda e, i=i: e.tensor_scalar(out=RG.ap[:, i * 128:(i + 1) * 128], in0=cst[:, R_OFF:R_OFF + 128],
                                                    scalar1=pp[:, PP_QKG + i:PP_QKG + i + 1], scalar2=None, op0=ALU.mult),
             reads=CSTB.res() + PPB.res(), writes=RG.res())

    wcnt = [0]

    def load_w(src):
        slot = wcnt[0] % NW
        wcnt[0] += 1
        P.op("pool", lambda e, slot=slot, src=src: e.dma_start(out=WR[slot].ap, in_=src),
             writes=WR[slot].res(), lane=f"w{slot}")
        return WR[slot]

    pcnt = [0]

    def next_bank(lo=0, n=8):
        b = lo + pcnt[0] % n
        pcnt[0] += 1
        return b

    dbg_names = []

    def dump_buf(name, ap, reads):
        if name in dump_out:
            n = ap.shape[1]
            for i in range(0, n, 1024):
                P.op("pool", lambda e, i=i: e.dma_start(out=dump_out[name][:, i:i + 1024], in_=ap[:, i:i + 1024]), reads=reads,
                     writes=["dbg_" + name + str(i)], lane="dbg")
                dbg_names.append("dbg_" + name + str(i))

    def rmsnorm(norm_idx, T0):
        SQ = [alloc(f"nsq{i}", BF16, TT) for i in range(2)]
        RS = [alloc(f"nrs{i}", F32, TT) for i in range(2)]
        for tt in range(NTT):
            b = next_bank()
            ts = slice(tt * TT, (tt + 1) * TT)
            for c in range(8):
                sq = SQ[c % 2]
                P.op("act", lambda e, sq=sq, c=c, ts=ts: e.activation(out=sq.ap, in_=Xv[:, c, ts], func=AF.Square),
                     reads=xres(c, tt), writes=sq.res())
                P.op("pe", lambda e, sq=sq, c=c, b=b: e.matmul(PS(b), lhsT=ones_bf, rhs=sq.ap, start=(c == 0), stop=(c == 7)),
                     reads=sq.res() + CSTB.res(ONES_OFF, ONES_OFF + 128), writes=PSR(b))
            rs = RS[tt % 2]
            P.op("act", lambda e, rs=rs, b=b: e.activation(out=rs.ap, in_=PS(b), func=AF.Ln, scale=1.0 / D, bias=pp[:, C_EPS:C_EPS + 1]),
                 reads=PSR(b) + PPB.res(), writes=rs.res())
            P.op("act", lambda e, rs=rs: e.activation(out=rs.ap, in_=rs.ap, func=AF.Exp, scale=-0.5),
                 reads=rs.res(), writes=rs.res())
            for c in range(8):
                gcol = PP_NORM + 8 * norm_idx + c
                P.op("dve", lambda e, rs=rs, c=c, ts=ts, gcol=gcol: e.scalar_tensor_tensor(
                    out=XNv[:, c, ts], in0=Xv[:, c, ts], scalar=pp[:, gcol:gcol + 1], in1=rs.ap, op0=ALU.mult, op1=ALU.mult),
                     reads=xres(c, tt) + rs.res() + PPB.res(), writes=xnres(c, tt))

    def proj_fm(wb, tt, b):
        ts = slice(tt * TT, (tt + 1) * TT)
        wv = wb.ap.rearrange("p (c n) -> p c n", c=8)

        def fn(e):
            ins = None
            for c in range(8):
                ins = e.matmul(PS(b), lhsT=wv[:, c, :], rhs=XNv[:, c, ts], start=(c == 0), stop=(c == 7))
            return ins
        P.op("pe", fn, reads=wb.res() + [r for c in range(8) for r in xnres(c, tt)], writes=PSR(b))

    def qk_post(b, tt, rg_idx, dst, T):
        ts = slice(tt * TT, (tt + 1) * TT)
        sq, qb, rs, t1, t2 = T
        gcol = PP_QKG + rg_idx
        P.op("act", lambda e: e.activation(out=sq.ap, in_=PS(b), func=AF.Square), reads=PSR(b), writes=sq.res())
        P.op("act", lambda e: e.activation(out=qb.ap, in_=PS(b), func=AF.Copy, scale=pp[:, gcol:gcol + 1]),
             reads=PSR(b) + PPB.res(), writes=qb.res())
        b1 = next_bank()
        P.op("pe", lambda e: e.matmul(PS(b1), lhsT=bones_bf, rhs=sq.ap, start=True, stop=True),
             reads=sq.res() + CSTB.res(BONES_OFF, BONES_OFF + 128), writes=PSR(b1))
        b2 = next_bank()
        P.op("pe", lambda e: e.matmul(PS(b2), lhsT=cst[:, R_OFF:R_OFF + 128], rhs=qb.ap, start=True, stop=True),
             reads=qb.res() + CSTB.res(R_OFF, R_OFF + 128), writes=PSR(b2))
        P.op("act", lambda e: e.activation(out=rs.ap, in_=PS(b1), func=AF.Ln, scale=1.0 / 64, bias=pp[:, C_EPS:C_EPS + 1]),
             reads=PSR(b1) + PPB.res(), writes=rs.res())
        P.op("act", lambda e: e.activation(out=rs.ap, in_=rs.ap, func=AF.Exp, scale=-0.5), reads=rs.res(), writes=rs.res())
        P.op("act", lambda e: e.activation(out=t2.ap, in_=PS(b2), func=AF.Copy), reads=PSR(b2), writes=t2.res())
        P.op("dve", lambda e: e.tensor_tensor(out=t1.ap, in0=qb.ap, in1=Ctab[:, ts], op=ALU.mult),
             reads=qb.res() + CSTB.res(C_OFF + tt * TT, C_OFF + (tt + 1) * TT), writes=t1.res())
        P.op("dve", lambda e: e.tensor_tensor(out=t2.ap, in0=t2.ap, in1=Stab[:, ts], op=ALU.mult),
             reads=t2.res() + CSTB.res(S_OFF + tt * TT, S_OFF + (tt + 1) * TT), writes=t2.res())
        P.op("dve", lambda e: e.tensor_tensor(out=t1.ap, in0=t1.ap, in1=t2.ap, op=ALU.add),
             reads=t1.res() + t2.res(), writes=t1.res())
        if isinstance(dst, tuple):
            for hf, dd in enumerate(dst):
                pr = slice(64 * hf, 64 * hf + 64)
                P.op("dve", lambda e, pr=pr, dd=dd: e.tensor_tensor(out=dd.ap[pr, ts], in0=t1.ap[pr, :], in1=rs.ap[pr, :], op=ALU.mult),
                     reads=t1.res() + rs.res(), writes=dd.res(tt * TT, (tt + 1) * TT))
        else:
            P.op("dve", lambda e: e.tensor_tensor(out=dst.ap[:, ts], in0=t1.ap, in1=rs.ap, op=ALU.mult),
                 reads=t1.res() + rs.res(), writes=dst.res(tt * TT, (tt + 1) * TT))

    def v_proj(wb, VA, vw, col0, ncols, segs):
        wv = wb.ap.rearrange("p (c n) -> p c n", c=8)
        VAv = VA.ap.rearrange("p (k w) -> p k w", w=vw)
        for g4 in range(NQB // 4):
            b = next_bank()

            def fn(e, g4=g4, b=b):
                ins = None
                for i in range(4):
                    blk = g4 * 4 + i
                    for c in range(8):
                        ins = e.matmul(PS(b)[:, i * 128:(i + 1) * 128], lhsT=XNv[:, c, blk * 128:(blk + 1) * 128], rhs=wv[:, c, :],
                                       start=(c == 0), stop=(c == 7))
                return ins
            P.op("pe", fn, reads=wb.res() + [r for c in range(8) for r in xnres(c, g4)], writes=PSR(b))
            psv = PS(b).rearrange("p (i n) -> p i n", i=4)
            for (slo, shi, dlo) in segs:
                P.op("act", lambda e, g4=g4, psv=psv, slo=slo, shi=shi, dlo=dlo: e.activation(
                    out=VAv[:, g4 * 4:(g4 + 1) * 4, dlo:dlo + (shi - slo)], in_=psv[:, :, slo:shi], func=AF.Copy),
                     reads=PSR(b), writes=VA.res(g4 * 4 * vw, (g4 + 1) * 4 * vw))

    def attention(QT, KT, VA, vw, dv, ET, units, mask_fn):
        VAv = VA.ap.rearrange("p (k w) -> p k w", w=vw)
        sbanks = [0, 1, 2]
        obanks = [4, 5, 6, 7]
        stages = []
        for (Q, prow, vcol, fin) in units:
            for j in range(4 * Q + 4):
                stages.append((Q, prow, vcol, fin, j))

        def stage_a(n):
            Q, prow, vcol, fin, j = stages[n]
            r = max(0, j - 4 * Q)
            lo = r * 128
            sb = sbanks[n % 3]
            et = ET[n % len(ET)]
            QZ = prow
            lm = mask_fn(et, Q, j, r)

            def fn(e):
                ins = e.matmul(PS(sb)[:, lo:TT], lhsT=KT.ap[:, j * 128:(j + 1) * 128],
                               rhs=QZ.ap[:, Q * TT + lo:(Q + 1) * TT], start=True, stop=(lm is None))
                if lm is not None:
                    plo, ncol, tlo = lm
                    ins = e.matmul(PS(sb)[:, plo:plo + ncol], lhsT=ident_bf, rhs=cst[:, tlo:tlo + ncol], start=False, stop=True)
                return ins
            P.op("pe", fn, reads=KT.res(j * 128, (j + 1) * 128) + QZ.res(Q * TT + lo, (Q + 1) * TT) + CSTB.res(ML1_OFF, IDENT_OFF + 128),
                 writes=PSR(sb))
            P.op("act", lambda e: e.activation(out=et.ap[:, lo:TT], in_=PS(sb)[:, lo:TT], func=AF.Exp, scale=0.125),
                 reads=PSR(sb), writes=et.res())

        def stage_b(n):
            Q, prow, vcol, fin, j = stages[n]
            r = max(0, j - 4 * Q)
            et = ET[n % len(ET)]
            for il in range(r, 4):
                i = 4 * Q + il
                ob = obanks[il]
                P.op("pe", lambda e, il=il, i=i, ob=ob: e.matmul(
                    PS(ob)[:, 0:dv + 1], lhsT=et.ap[:, il * 128:(il + 1) * 128], rhs=VAv[:, j, vcol:vcol + dv + 1],
                    start=(j == 0), stop=(j == i)),
                     reads=et.res() + VA.res(j * vw, (j + 1) * vw), writes=PSR(ob))
                if j == i:
                    fin(i, ob)

        LOOK = 3
        for n in range(min(LOOK, len(stages))):
            stage_a(n)
        for n in range(len(stages)):
            stage_b(n)
            if n + LOOK < len(stages):
                stage_a(n + LOOK)

    def stop(tag):
        if stop_after == tag:
            raise _Stop()

    def run_seq(s):
        for c in range(8):
            P.op("sp", lambda e, c=c, s=s: e.dma_start(out=Xv[:, c, :], in_=xT[s, c * 128:(c + 1) * 128, :]),
                 writes=X.res(c * S, (c + 1) * S), lane=f"io{c % 2}")

        scr_alloc_reset()
        rmsnorm(0, None)
        if s == 0:
            dump_buf("xn0", XN.ap, XN.res())
        scr_alloc_reset()
        XR = alloc("XR", F32, S + 4)
        U = alloc("U", F32, S)
        RB = alloc("RB", F32, S)
        IB = alloc("IB", F32, S)
        GG = alloc("GG", BF16, S)
        UB = alloc("UB", BF16, S)
        P.op("dve", lambda e: e.memset(XR.ap[:, 0:4], 0.0), writes=XR.res(0, 4))
        for c4 in range(4):
            wxr = load_w(w_in[c4])
            wgt = load_w(w_in[4 + c4])
            for tt in range(NTT):
                ts = slice(tt * TT, (tt + 1) * TT)
                b = next_bank()
                proj_fm(wxr, tt, b)
                P.op("act", lambda e, b=b, tt=tt: e.activation(out=XR.ap[:, 4 + tt * TT:4 + (tt + 1) * TT], in_=PS(b), func=AF.Copy),
                     reads=PSR(b), writes=XR.res(4 + tt * TT, 4 + (tt + 1) * TT))
                b = next_bank()
                proj_fm(wgt, tt, b)
                P.op("act", lambda e, b=b, ts=ts: e.activation(out=GG.ap[:, ts], in_=PS(b), func=AF.Gelu_apprx_tanh),
                     reads=PSR(b), writes=GG.res(tt * TT, (tt + 1) * TT))
            cw = PP_CONVW + 4 * c4
            P.op("dve", lambda e, cw=cw, c4=c4: e.tensor_scalar(out=U.ap, in0=XR.ap[:, 4:4 + S], scalar1=pp[:, cw + 3:cw + 4],
                                                                 scalar2=pp[:, PP_CONVB + c4:PP_CONVB + c4 + 1], op0=ALU.mult, op1=ALU.add),
                 reads=XR.res() + PPB.res(), writes=U.res())
            for j in range(3):
                P.op("dve", lambda e, cw=cw, j=j: e.scalar_tensor_tensor(out=U.ap, in0=XR.ap[:, 1 + j:1 + j + S], scalar=pp[:, cw + j:cw + j + 1],
                                                                         in1=U.ap, op0=ALU.mult, op1=ALU.add),
                     reads=XR.res() + U.res() + PPB.res(), writes=U.res())
            P.op("dve", lambda e: e.tensor_copy(out=UB.ap, in_=U.ap), reads=U.res(), writes=UB.res())
            for tt in range(NTT):
                ts = slice(tt * TT, (tt + 1) * TT)
                for (gi, dstb, bcol) in ((0, RB, PP_BA), (1, IB, PP_BX)):
                    b = next_bank()
                    P.op("pe", lambda e, b=b, gi=gi, c4=c4, ts=ts: e.matmul(PS(b), lhsT=BD.ap[:, (4 * gi + c4) * 128:(4 * gi + c4 + 1) * 128],
                                                                             rhs=UB.ap[:, ts], start=True, stop=True),
                         reads=BD.res() + UB.res(tt * TT, (tt + 1) * TT), writes=PSR(b))
                    P.op("act", lambda e, b=b, dstb=dstb, ts=ts, bcol=bcol, c4=c4: e.activation(
                        out=dstb.ap[:, ts], in_=PS(b), func=AF.Sigmoid, bias=pp[:, bcol + c4:bcol + c4 + 1]),
                         reads=PSR(b) + PPB.res(), writes=dstb.res(tt * TT, (tt + 1) * TT))
            Tap = XR.ap[:, 4:4 + S]
            P.op("act", lambda e, c4=c4: e.activation(out=Tap, in_=RB.ap, func=AF.Exp, scale=pp[:, C_NSP2 + c4:C_NSP2 + c4 + 1]),
                 reads=RB.res() + PPB.res(), writes=XR.res())
            P.op("act", lambda e, c4=c4: e.activation(out=RB.ap, in_=RB.ap, func=AF.Exp, scale=pp[:, C_NSP + c4:C_NSP + c4 + 1]),
                 reads=RB.res() + PPB.res(), writes=RB.res())
            P.op("act", lambda e: e.activation(out=Tap, in_=Tap, func=AF.Sqrt, scale=-1.0, bias=1.0), reads=XR.res(), writes=XR.res())
            P.op("dve", lambda e: e.tensor_tensor(out=IB.ap, in0=IB.ap, in1=U.ap, op=ALU.mult), reads=IB.res() + U.res(), writes=IB.res())
            P.op("dve", lambda e: e.tensor_tensor(out=IB.ap, in0=IB.ap, in1=Tap, op=ALU.mult), reads=IB.res() + XR.res(), writes=IB.res())
            P.op("dve", lambda e: e.tensor_tensor_scan(out=U.ap, data0=RB.ap, data1=IB.ap, initial=0.0, op0=ALU.mult, op1=ALU.add),
                 reads=RB.res() + IB.res(), writes=U.res())
            P.op("dve", lambda e, c4=c4: e.tensor_tensor(out=Yv[:, c4, :], in0=U.ap, in1=GG.ap, op=ALU.mult),
                 reads=U.res() + GG.res(), writes=yres(c4, 0, S))
        if s == 0:
            dump_buf("y0", Y.ap, Y.res())
        stop("rnn")
        scr_alloc_reset()
        QT = alloc("QT", BF16, S)
        QZ1 = alloc("QZ1", BF16, S)
        KT = alloc("KT", BF16, S)
        VW0 = 132
        VA = alloc("VA", BF16, NQB * VW0)
        ET = [alloc(f"ET{i}", BF16, TT) for i in range(5)]
        P.op("dve", lambda e: e.memset(QT.ap[64:128, :], 0.0), writes=QT.res())
        P.op("dve", lambda e: e.memset(QZ1.ap[0:64, :], 0.0), writes=QZ1.res())
        TQ = (alloc("tsq", BF16, TT), alloc("tqb", BF16, TT), alloc("trs", F32, TT), alloc("tt1", F32, TT), alloc("tt2", F32, TT))
        TQb = (alloc("tsqb", BF16, TT), alloc("tqbb", BF16, TT), alloc("trsb", F32, TT), alloc("tt1b", F32, TT), alloc("tt2b", F32, TT))
        OST = alloc("OST", F32, 132)
        OS = alloc("OS", BF16, 4 * 128)
        SM = alloc("SM", F32, 8)
        OAS = alloc("OAS", F32, 4 * 128)
        VAv0 = VA.ap.rearrange("p (k w) -> p k w", w=VW0)
        nxt0 = (load_w(w_in[8]), load_w(w_in[12]), load_w(w_in[16]))
        for h in range(4):
            wq, wk, wvv = nxt0
            if h < 3:
                nxt0 = (load_w(w_in[8 + h + 1]), load_w(w_in[12 + h + 1]), load_w(w_in[16 + h + 1]))
            for tt in range(NTT):
                b = next_bank()
                proj_fm(wq, tt, b)
                qk_post(b, tt, 0, (QT, QZ1), TQ)
                b = next_bank()
                proj_fm(wk, tt, b)
                qk_post(b, tt, 1, KT, TQb)
            if s == 0 and h == 0:
                dump_buf("qt0", QT.ap, QT.res())
                dump_buf("kt0", KT.ap, KT.res())
            stop("qk")
            v_proj(wvv, VA, VW0, 0, 128, [(0, 128, 0)])
            P.op("dve", lambda e: e.memset(VAv0[:, :, 128:129], 1.0), writes=VA.res())
            stop("vproj")

            def mask0(et, Q, j, r):
                if j >= 4 * Q:
                    return (r * 128, 128, MCUR_OFF)
                return None

            def fin_b(i, ob, h=h):
                il = i % 4
                P.op("act", lambda e, ob=ob: e.activation(out=OST.ap[:, 0:129], in_=PS(ob)[:, 0:129], func=AF.Copy), reads=PSR(ob), writes=OST.res())
                P.op("dve", lambda e, ob=ob: e.reciprocal(out=SM.ap[:, 1:2], in_=OST.ap[:, 128:129]), reads=OST.res(), writes=SM.res())
                P.op("dve", lambda e: e.tensor_tensor(out=SM.ap[:, 1:2], in0=SM.ap[:, 1:2], in1=pp[:, C_NLAM:C_NLAM + 1], op=ALU.mult),
                     reads=SM.res() + PPB.res(), writes=SM.res())
                P.op("dve", lambda e, ob=ob, i=i, il=il: e.scalar_tensor_tensor(out=OS.ap[:, il * 128:(il + 1) * 128], in0=OST.ap[:, 0:128], scalar=SM.ap[:, 1:2],
                                                                                in1=OAS.ap[:, il * 128:(il + 1) * 128], op0=ALU.mult, op1=ALU.add),
                     reads=OST.res() + SM.res() + OAS.res(il * 128, (il + 1) * 128), writes=OS.res())
                if il == 3:
                    Q = i // 4
                    tb = 3
                    psb = PS(tb).bitcast(BF16)
                    sq, qb, rs, t1, t2 = TQ

                    def fn(e):
                        ins = None
                        for k in range(4):
                            ins = e.transpose(psb[:, k * 128:(k + 1) * 128], OS.ap[:, k * 128:(k + 1) * 128], ident_bf)
                        return ins
                    P.op("pe", fn, reads=OS.res() + CSTB.res(IDENT_OFF, IDENT_OFF + 128), writes=PSR(tb))
                    P.op("act", lambda e: e.activation(out=t1.ap, in_=psb[:, 0:TT], func=AF.Copy), reads=PSR(tb), writes=t1.res())
                    P.op("act", lambda e: e.activation(out=sq.ap, in_=psb[:, 0:TT], func=AF.Square), reads=PSR(tb), writes=sq.res())
                    P.op("pe", lambda e: e.matmul(PS(tb), lhsT=ones_bf, rhs=sq.ap, start=True, stop=True),
                         reads=sq.res() + CSTB.res(ONES_OFF, ONES_OFF + 128), writes=PSR(tb))
                    P.op("act", lambda e: e.activation(out=rs.ap, in_=PS(tb), func=AF.Ln, scale=1.0 / 128, bias=pp[:, C_EPS:C_EPS + 1]),
                         reads=PSR(tb) + PPB.res(), writes=rs.res())
                    P.op("act", lambda e: e.activation(out=rs.ap, in_=rs.ap, func=AF.Exp, scale=-0.5), reads=rs.res(), writes=rs.res())
                    P.op("dve", lambda e, Q=Q, h=h: e.scalar_tensor_tensor(out=Yv[:, 4 + h, Q * TT:(Q + 1) * TT], in0=t1.ap, scalar=pp[:, C_SUBG:C_SUBG + 1],
                                                                            in1=rs.ap, op0=ALU.mult, op1=ALU.mult),
                         reads=t1.res() + rs.res() + PPB.res(), writes=yres(4 + h, Q * TT, (Q + 1) * TT))

            def fin_a2(i, ob):
                P.op("act", lambda e, ob=ob: e.activation(out=OST.ap[:, 0:129], in_=PS(ob)[:, 0:129], func=AF.Copy), reads=PSR(ob), writes=OST.res())
                P.op("dve", lambda e, ob=ob: e.reciprocal(out=SM.ap[:, 0:1], in_=OST.ap[:, 128:129]), reads=OST.res(), writes=SM.res())
                P.op("dve", lambda e, ob=ob, i=i: e.tensor_scalar(out=OAS.ap[:, (i % 4) * 128:(i % 4 + 1) * 128], in0=OST.ap[:, 0:128], scalar1=SM.ap[:, 0:1],
                                                                   scalar2=None, op0=ALU.mult),
                     reads=OST.res() + SM.res(), writes=OAS.res((i % 4) * 128, (i % 4 + 1) * 128))

            units = [(Q, (QT, QZ1)[c], 0, (fin_a2, fin_b)[c]) for Q in range(NTT) for c in range(2)]
            if stop_after == "att_a":
                attention(QT, KT, VA, VW0, 128, ET, units[:NTT], mask0)
                stop("att_a")
            attention(QT, KT, VA, VW0, 128, ET, units, mask0)
            stop("att_b")
        if s == 0:
            dump_buf("y0b", Y.ap, Y.res())
        stop("attn0")

        def out_proj(wdram):
            for oc in range(8):
                wb = load_w(wdram[oc])
                wv = wb.ap.rearrange("p (c n) -> p c n", c=8)
                for tt in range(NTT):
                    ts = slice(tt * TT, (tt + 1) * TT)
                    b = next_bank()

                    def fn(e, wv=wv, ts=ts, b=b):
                        ins = None
                        for c in range(8):
                            ins = e.matmul(PS(b), lhsT=wv[:, c, :], rhs=Yv[:, c, ts], start=(c == 0), stop=(c == 7))
                        return ins
                    P.op("pe", fn, reads=wb.res() + [r for c in range(8) for r in yres(c, tt * TT, (tt + 1) * TT)], writes=PSR(b))
                    ev = EV[evc[0] % 2]
                    evc[0] += 1
                    P.op("act", lambda e, b=b, ev=ev: e.activation(out=ev.ap, in_=PS(b), func=AF.Copy), reads=PSR(b), writes=ev.res())
                    P.op("dve", lambda e, oc=oc, ts=ts, ev=ev: e.tensor_tensor(out=Xv[:, oc, ts], in0=Xv[:, oc, ts], in1=ev.ap, op=ALU.add),
                         reads=ev.res() + xres(oc, tt), writes=xres(oc, tt))

        scr_alloc_reset()
        EV = [alloc(f"ev{i}", F32, TT) for i in range(2)]
        evc = [0]
        out_proj(w_out0)
        if s == 0:
            dump_buf("x1", X.ap, X.res())
        stop("mix0")

        def ffn(l):
            scr_alloc_reset()
            rmsnorm(1 + 2 * l, None)
            scr_alloc_reset()
            HQ = 4
            H = alloc("H", BF16, HQ * S)
            Hv = H.ap.rearrange("p (k t) -> p k t", k=HQ)
            SG = [alloc(f"sg{i}", F32, TT) for i in range(2)]
            SU = [alloc(f"su{i}", F32, TT) for i in range(2)]
            EVF = [alloc(f"evf{i}", F32, TT) for i in range(2)]
            groups = [list(range(g, min(g + HQ, NHB))) for g in range(0, NHB, HQ)]
            for grp in groups:
                for k, hb in enumerate(grp):
                    wg = load_w(w_gate[l][hb])
                    wu = load_w(w_up[l][hb])
                    for tt in range(NTT):
                        ts = slice(tt * TT, (tt + 1) * TT)
                        bg = next_bank()
                        proj_fm(wg, tt, bg)
                        bu = next_bank()
                        proj_fm(wu, tt, bu)
                        sg = SG[tt % 2]
                        P.op("act", lambda e, bg=bg, sg=sg: e.activation(out=sg.ap, in_=PS(bg), func=AF.Silu), reads=PSR(bg), writes=sg.res())
                        su = SU[tt % 2]
                        P.op("act", lambda e, bu=bu, su=su: e.activation(out=su.ap, in_=PS(bu), func=AF.Copy), reads=PSR(bu), writes=su.res())
                        P.op("dve", lambda e, su=su, sg=sg, k=k, ts=ts: e.tensor_tensor(out=Hv[:, k, ts], in0=sg.ap, in1=su.ap, op=ALU.mult),
                             reads=su.res() + sg.res(), writes=H.res(k * S + tt * TT, k * S + (tt + 1) * TT))
                wds = [load_w(w_down[l][hb]) for hb in grp]
                for oc in range(8):
                    for tt in range(NTT):
                        ts = slice(tt * TT, (tt + 1) * TT)
                        b = next_bank()

                        def fn(e, oc=oc, ts=ts, b=b, n=len(grp), wds=wds):
                            ins = None
                            for k in range(n):
                                ins = e.matmul(PS(b), lhsT=wds[k].ap[:, oc * 128:(oc + 1) * 128], rhs=Hv[:, k, ts], start=(k == 0), stop=(k == n - 1))
                            return ins
                        P.op("pe", fn, reads=[r for w in wds for r in w.res()] + [r for k in range(len(grp)) for r in H.res(k * S + tt * TT, k * S + (tt + 1) * TT)],
                             writes=PSR(b))
                        ev = EVF[(oc * NTT + tt) % 2]
                        P.op("act", lambda e, b=b, ev=ev: e.activation(out=ev.ap, in_=PS(b), func=AF.Copy), reads=PSR(b), writes=ev.res())
                        P.op("dve", lambda e, oc=oc, ts=ts, ev=ev: e.tensor_tensor(out=Xv[:, oc, ts], in0=Xv[:, oc, ts], in1=ev.ap, op=ALU.add),
                             reads=ev.res() + xres(oc, tt), writes=xres(oc, tt))

        ffn(0)
        if s == 0:
            dump_buf("x2", X.ap, X.res())
        stop("ffn0")

        scr_alloc_reset()
        rmsnorm(2, None)
        scr_alloc_reset()
        QT = alloc("QT1", BF16, S)
        QZ1 = alloc("QZ11", BF16, S)
        KT = alloc("KT1", BF16, S)
        VW1 = 132
        VA = alloc("VA1", BF16, NQB * VW1)
        ET = [alloc(f"ET1{i}", BF16, TT) for i in range(6)]
        P.op("dve", lambda e: e.memset(QT.ap[64:128, :], 0.0), writes=QT.res())
        P.op("dve", lambda e: e.memset(QZ1.ap[0:64, :], 0.0), writes=QZ1.res())
        TQ = (alloc("tsq1", BF16, TT), alloc("tqb1", BF16, TT), alloc("trs1", F32, TT), alloc("tt11", F32, TT), alloc("tt21", F32, TT))
        OS = alloc("OS1", BF16, 4 * 128)
        SM = alloc("SM1", F32, 8)
        OST = alloc("OST1", F32, 132)
        VAv1 = VA.ap.rearrange("p (k w) -> p k w", w=VW1)
        nxt1 = (load_w(w_qkv[0]), load_w(w_qkv[8]), load_w(w_qkv[16]))
        TQ2 = (alloc("tsq2", BF16, TT), alloc("tqb2", BF16, TT), alloc("trs2", F32, TT), alloc("tt12", F32, TT), alloc("tt22", F32, TT))
        for hp in range(8):
            wq, wk, wvv = nxt1
            if hp < 7:
                nxt1 = (load_w(w_qkv[hp + 1]), load_w(w_qkv[8 + hp + 1]), load_w(w_qkv[16 + hp + 1]))
            for tt in range(NTT):
                b = next_bank()
                proj_fm(wq, tt, b)
                qk_post(b, tt, 2, (QT, QZ1), TQ)
                b = next_bank()
                proj_fm(wk, tt, b)
                qk_post(b, tt, 3, KT, TQ2)
            v_proj(wvv, VA, VW1, 0, 128, [(0, 64, 0), (64, 128, 65)])
            P.op("dve", lambda e: e.memset(VAv1[:, :, 64:65], 1.0), writes=VA.res())
            P.op("dve", lambda e: e.memset(VAv1[:, :, 129:130], 1.0), writes=VA.res())

            def mask1(et, Q, j, r):
                d0 = 4 * Q + r - j
                return (r * 128, (4 - r) * 128, ML1_OFF + d0 * 128)

            def make_fin(h2, hp=hp):
                def fin(i, ob):
                    il = i % 4
                    P.op("act", lambda e, ob=ob: e.activation(out=OST.ap[:, 0:65], in_=PS(ob)[:, 0:65], func=AF.Copy), reads=PSR(ob), writes=OST.res())
                    P.op("dve", lambda e, ob=ob: e.reciprocal(out=SM.ap[:, h2:h2 + 1], in_=OST.ap[:, 64:65]), reads=OST.res(), writes=SM.res())
                    P.op("dve", lambda e, ob=ob, il=il: e.tensor_scalar(out=OS.ap[:, il * 128 + h2 * 64:il * 128 + h2 * 64 + 64], in0=OST.ap[:, 0:64],
                                                                         scalar1=SM.ap[:, h2:h2 + 1], scalar2=None, op0=ALU.mult),
                         reads=OST.res() + SM.res(), writes=OS.res())
                    if il == 3 and h2 == 1:
                        Q = i // 4
                        tb = 3
                        psb = PS(tb).bitcast(BF16)

                        def fn(e):
                            ins = None
                            for k in range(4):
                                ins = e.transpose(psb[:, k * 128:(k + 1) * 128], OS.ap[:, k * 128:(k + 1) * 128], ident_bf)
                            return ins
                        P.op("pe", fn, reads=OS.res() + CSTB.res(IDENT_OFF, IDENT_OFF + 128), writes=PSR(tb))
                        P.op("act", lambda e, Q=Q: e.activation(out=Yv[:, hp, Q * TT:(Q + 1) * TT], in_=psb[:, 0:TT], func=AF.Copy),
                             reads=PSR(tb), writes=yres(hp, Q * TT, (Q + 1) * TT))
                return fin
            fins = [make_fin(0), make_fin(1)]
            units = [(Q, (QT, QZ1)[h2], 65 * h2, fins[h2]) for Q in range(NTT) for h2 in range(2)]
            attention(QT, KT, VA, VW1, 64, ET, units, mask1)
        if s == 0:
            dump_buf("y1", Y.ap, Y.res())
        stop("attn1")
        scr_alloc_reset()
        EV = [alloc(f"ev{i}", F32, TT) for i in range(2)]
        out_proj(w_out1)
        if s == 0:
            dump_buf("x3", X.ap, X.res())
        ffn(1)
        for c in range(8):
            P.op("sp", lambda e, c=c, s=s: e.dma_start(out=oT[s, c * 128:(c + 1) * 128, :], in_=Xv[:, c, :]),
                 reads=X.res(c * S, (c + 1) * S), writes=[f"out{s}_{c}"], lane=f"io{c % 2}")
    try:
        for s in range(nseq):
            run_seq(s)
    except _Stop:
        pass
    P.op("sp", lambda e: e.nop(), reads=[f"out{s}_{c}" for s in range(nseq) for c in range(8)] + dbg_names)
    P.emit()
    return nc, P


_CACHE = {}


def kernel(**inputs):
    x = np.asarray(inputs["x"], dtype=np.float32)
    sh = prep_shared(inputs)
    if "nc" not in _CACHE:
        _CACHE["nc"] = build_program(2)[0]
    nc = _CACHE["nc"]
    in_maps = []
    for c in range(NCORES):
        m = dict(sh)
        m["xT"] = np.ascontiguousarray(x[2 * c:2 * c + 2].transpose(0, 2, 1))
        in_maps.append(m)
    res = run_bass_kernel_spmd(nc, in_maps, core_ids=list(range(NCORES)))
    out = np.empty_like(x)
    for c in range(NCORES):
        out[2 * c:2 * c + 2] = np.asarray(res.results[c]["oT"]).transpose(0, 2, 1)
    return out
```

```python
import math
import contextlib
import numpy as np
import concourse.bass as bass
import concourse.mybir as mybir
from concourse.bass_utils import run_bass_kernel_spmd

F32 = mybir.dt.float32
BF16 = mybir.dt.bfloat16
AF = mybir.ActivationFunctionType
ALU = mybir.AluOpType
AX = mybir.AxisListType

NCORES = 8
S = 2048
D = 1024
TT = 512
NTT = S // TT
NQB = S // 128
DFF = 2816
NHB = DFF // 128
EPS = 1e-6
EPOCH = 1500
GR = 512


class Op:
    pass


class _Stop(Exception):
    pass


class Prog:
    ENGS = ("pe", "act", "dve", "pool", "sp")

    def __init__(self, nc):
        self.nc = nc
        self.ops = {e: [] for e in self.ENGS}
        self.lanes = {}
        self.last_w = {}
        self.readers = {}

    def _tok(self, o):
        return ("l", o.lane, o.lidx) if o.is_dma else ("e", o.eng, o.idx)

    def op(self, eng, fn, reads=(), writes=(), lane=None):
        o = Op()
        o.eng, o.fn, o.lane = eng, fn, lane
        o.is_dma = lane is not None
        o.deps, o.raw, o.signal, o.sigidx = set(), set(), False, None
        o.idx = len(self.ops[eng])
        self.ops[eng].append(o)
        if o.is_dma:
            L = self.lanes.setdefault(lane, [])
            o.lidx = len(L)
            if L:
                o.deps.add(self._tok(L[-1]))
            L.append(o)
        tok = self._tok(o)
        for r in reads:
            lw = self.last_w.get(r)
            if lw is not None:
                o.deps.add(lw)
                o.raw.add(lw)
        for w in writes:
            lw = self.last_w.get(w)
            if lw is not None:
                o.deps.add(lw)
            rl = self.readers.get(w)
            if rl:
                o.deps.update(rl)
        for r in reads:
            self.readers.setdefault(r, []).append(tok)
        for w in writes:
            self.last_w[w] = tok
            self.readers[w] = []
        o.deps.discard(tok)
        return o

    def emit(self):
        nc = self.nc
        known = {e: {} for e in self.ENGS}
        waits = {}
        for e in self.ENGS:
            kn = known[e]
            for o in self.ops[e]:
                need = {}
                for (kind, key, idx) in o.deps:
                    if kind == "e" and key == e and not o.is_dma:
                        if e in ("pe", "sp"):
                            continue
                        if (kind, key, idx) not in o.raw or o.idx - idx > 3:
                            continue
                    k = (kind, key)
                    if kn.get(k, -1) >= idx:
                        continue
                    if need.get(k, -1) < idx:
                        need[k] = idx
                wl = []
                for k, idx in need.items():
                    kn[k] = idx
                    wl.append((k[0], k[1], idx))
                    if k[0] == "e":
                        self.ops[k[1]][idx].signal = True
                waits[o] = wl
        nsig, cum = {}, {}
        for e in self.ENGS:
            c, arr = 0, []
            for o in self.ops[e]:
                if o.signal and not o.is_dma:
                    o.sigidx = c
                    c += 1
                arr.append(c)
            nsig[e], cum[e] = c, arr
        sems = {}
        st = contextlib.ExitStack()
        for e in self.ENGS:
            for ep in range(max((nsig[e] + EPOCH - 1) // EPOCH, 1)):
                sems[("e", e, ep)] = st.enter_context(nc.semaphore(f"s_{e}_{ep}"))
        for ln in self.lanes:
            sems[("l", ln)] = st.enter_context(nc.semaphore(f"l_{ln}"))
        self.n_sems = len(sems)
        block = st.enter_context(nc.Block())
        engmap = {"pe": block.tensor, "act": block.scalar, "dve": block.vector,
                  "pool": block.gpsimd, "sp": block.sync}

        def make_body(e):
            def body(eng):
                for o in self.ops[e]:
                    for (kind, key, idx) in waits[o]:
                        if kind == "e":
                            c = cum[key][idx]
                            ep = (c - 1) // EPOCH
                            eng.wait_ge(sems[("e", key, ep)], c - ep * EPOCH)
                        else:
                            eng.wait_ge(sems[("l", key)], 16 * (idx + 1))
                    ins = o.fn(eng)
                    if o.is_dma:
                        ins.then_inc(sems[("l", o.lane)], 16)
                    elif o.signal:
                        ins.then_inc(sems[("e", e, o.sigidx // EPOCH)], 1)
            return body

        for e in self.ENGS:
            if self.ops[e]:
                engmap[e](make_body(e))
        st.close()


class Buf:
    def __init__(self, arena_bf, off, dtype, cols, name):
        self.off, self.dtype, self.cols, self.name = off, dtype, cols, name
        self.esz = 4 if dtype == F32 else 2
        v = arena_bf[:, off // 2: off // 2 + (cols * self.esz) // 2]
        self.ap = v.bitcast(F32) if dtype == F32 else v

    def res(self, lo=0, hi=None):
        hi = self.cols if hi is None else hi
        b0 = self.off + lo * self.esz
        b1 = self.off + hi * self.esz
        return [f"g{g}" for g in range(b0 // GR, (b1 - 1) // GR + 1)]


CST_COLS = 2 * S + 16 * 128 + 5 * 128
C_OFF, S_OFF, ML1_OFF = 0, S, 2 * S
MCUR_OFF = ML1_OFF + 16 * 128
ONES_OFF = MCUR_OFF + 128
BONES_OFF = ONES_OFF + 128
IDENT_OFF = BONES_OFF + 128
R_OFF = IDENT_OFF + 128


def make_consts():
    cst = np.zeros((128, CST_COLS), np.float32)
    pos = np.arange(S, dtype=np.float32)
    inv_freq = (1.0 / (np.float32(500000.0) ** (np.arange(0, 16, 2, dtype=np.float32) / np.float32(16)))).astype(np.float32)
    ang = pos[None, :] * inv_freq[:, None]
    cos, sin = np.cos(ang).astype(np.float32), np.sin(ang).astype(np.float32)
    C = np.ones((128, S), np.float32)
    Sg = np.zeros((128, S), np.float32)
    R = np.zeros((128, 128), np.float32)
    for g in range(2):
        for d in range(8):
            C[64 * g + d] = cos[d]
            C[64 * g + 8 + d] = cos[d]
            Sg[64 * g + d] = -sin[d]
            Sg[64 * g + 8 + d] = sin[d]
            R[64 * g + 8 + d, 64 * g + d] = 1.0
            R[64 * g + d, 64 * g + 8 + d] = 1.0
    cst[:, C_OFF:C_OFF + S] = C
    cst[:, S_OFF:S_OFF + S] = Sg
    k = np.arange(128)[:, None]
    q = np.arange(128)[None, :]
    for dl in range(16):
        dist = 128 * dl + q - k
        m = ((dist >= 0) & (dist <= 128)).astype(np.float32)
        m += ((dist >= 0) & (dist <= 512) & (dist % 4 == 0)).astype(np.float32)
        m += ((dist >= 0) & (dist % 16 == 0)).astype(np.float32)
        lm = np.where(m > 0, 8.0 * np.log(np.maximum(m, 1.0)), -1000.0).astype(np.float32)
        cst[:, ML1_OFF + 128 * dl: ML1_OFF + 128 * (dl + 1)] = lm
    cst[:, MCUR_OFF:MCUR_OFF + 128] = np.where(q >= k, 0.0, -1000.0).astype(np.float32)
    cst[:, ONES_OFF:ONES_OFF + 128] = 1.0
    bo = np.zeros((128, 128), np.float32)
    bo[:64, :64] = 1.0
    bo[64:, 64:] = 1.0
    cst[:, BONES_OFF:BONES_OFF + 128] = bo
    cst[:, IDENT_OFF:IDENT_OFF + 128] = np.eye(128, dtype=np.float32)
    cst[:, R_OFF:R_OFF + 128] = R
    return cst


PP_NORM = 0
PP_CONVW = 32
PP_CONVB = 48
PP_BA = 52
PP_BX = 56
PP_LAM = 60
PP_QKG = 64
PP_SUBG = 68
PP_COLS = 69
PF_COLS = 128 + 4 * 64


def relayout_kn(W):
    K, N = W.shape
    return np.ascontiguousarray(W.reshape(K // 128, 128, N // 128, 128).transpose(2, 1, 0, 3)).reshape(N // 128, 128, (K // 128) * 128)


def prep_shared(inp):
    f = lambda a: np.asarray(a, dtype=np.float32)
    sh = {}
    sh["w_in"] = relayout_kn(f(inp["ab_w_in"])[0])
    sh["w_out0"] = relayout_kn(f(inp["ab_w_out"])[0])
    sh["w_qkv"] = relayout_kn(f(inp["c_w_qkv"])[0])
    sh["w_out1"] = relayout_kn(f(inp["c_w_out"])[0])
    for l in range(2):
        sh[f"w_gate{l}"] = relayout_kn(f(inp["ffn_w_gate"])[l])
        sh[f"w_up{l}"] = relayout_kn(f(inp["ffn_w_up"])[l])
        sh[f"w_down{l}"] = np.ascontiguousarray(f(inp["ffn_w_down"])[l].reshape(NHB, 128, D))
    pp = np.zeros((128, PP_COLS), np.float32)
    for i, g in enumerate([f(inp["ab_norm_g"])[0], f(inp["ffn_norm_g"])[0], f(inp["c_norm_g"])[0], f(inp["ffn_norm_g"])[1]]):
        pp[:, PP_NORM + 8 * i: PP_NORM + 8 * i + 8] = g.reshape(8, 128).T
    cw = f(inp["ab_conv_w"])[0]
    for c4 in range(4):
        pp[:, PP_CONVW + 4 * c4: PP_CONVW + 4 * c4 + 4] = cw[:, 128 * c4:128 * c4 + 128].T
    pp[:, PP_CONVB:PP_CONVB + 4] = f(inp["ab_conv_b"])[0].reshape(4, 128).T
    pp[:, PP_BA:PP_BA + 4] = f(inp["ab_ba"])[0].reshape(4, 128).T
    pp[:, PP_BX:PP_BX + 4] = f(inp["ab_bx"])[0].reshape(4, 128).T
    pp[:, PP_LAM:PP_LAM + 4] = f(inp["ab_lru_lambda"])[0].reshape(4, 128).T
    for i, nm in enumerate(["ab_q_norm_g", "ab_k_norm_g", "c_q_norm_g", "c_k_norm_g"]):
        pp[:, PP_QKG + i] = np.tile(f(inp[nm])[0], 2)
    pp[:, PP_SUBG] = f(inp["ab_subln_g"])[0]
    sh["pp"] = pp
    pf = np.zeros((128, PF_COLS), np.float32)
    pf[:, 0:128] = f(inp["ab_subln_g"])[0][None, :]
    for i, nm in enumerate(["ab_lambda_q1", "ab_lambda_k1", "ab_lambda_q2", "ab_lambda_k2"]):
        pf[:, 128 + 64 * i: 128 + 64 * (i + 1)] = f(inp[nm])[0][None, :]
    sh["pf"] = pf
    for nm, key in (("wa_bd", "ab_wa"), ("wx_bd", "ab_wx")):
        w = f(inp[key])[0]
        bd = np.zeros((4, 128, 128), np.float32)
        for c4 in range(4):
            bd[c4, :64, :64] = w[2 * c4]
            bd[c4, 64:, 64:] = w[2 * c4 + 1]
        sh[nm] = bd
    sh["cst"] = make_consts()
    return sh


def build_program(nseq=2, dump=None, stop_after=None):
    dump = dump or {}
    nc = bass.Bass("TRN2", target_bir_lowering=False)
    P = Prog(nc)

    def din(name, shape):
        return nc.dram_tensor(name, list(shape), F32, kind="ExternalInput").ap()

    xT = din("xT", [nseq, D, S])
    oT = nc.dram_tensor("oT", [nseq, D, S], F32, kind="ExternalOutput").ap()
    w_in = din("w_in", [20, 128, 1024])
    w_out0 = din("w_out0", [8, 128, 1024])
    w_qkv = din("w_qkv", [24, 128, 1024])
    w_out1 = din("w_out1", [8, 128, 1024])
    w_gate = [din(f"w_gate{l}", [NHB, 128, 1024]) for l in range(2)]
    w_up = [din(f"w_up{l}", [NHB, 128, 1024]) for l in range(2)]
    w_down = [din(f"w_down{l}", [NHB, 128, 1024]) for l in range(2)]
    pp_d = din("pp", [128, PP_COLS])
    pf_d = din("pf", [128, PF_COLS])
    wa_d = din("wa_bd", [4, 128, 128])
    wx_d = din("wx_bd", [4, 128, 128])
    cst_d = din("cst", [128, CST_COLS])
    dump_out = {}
    for nm, shp in dump.items():
        dump_out[nm] = nc.dram_tensor("dbg_" + nm, list(shp), F32, kind="ExternalOutput").ap()

    ARENA_BYTES = 206 * 1024
    arena = nc.alloc_sbuf_tensor("arena", [128, ARENA_BYTES // 2], BF16).ap()
    cur = [0]

    def alloc(name, dtype, cols):
        esz = 4 if dtype == F32 else 2
        nb = ((cols * esz + GR - 1) // GR) * GR
        off = cur[0]
        cur[0] += nb
        assert cur[0] <= ARENA_BYTES, (name, cur[0])
        return Buf(arena, off, dtype, cols, name)

    X = alloc("X", F32, 8 * S)
    XN = alloc("XN", BF16, 8 * S)
    Y = alloc("Y", BF16, 8 * S)
    NW = 8
    WR = [alloc(f"W{i}", BF16, 1024) for i in range(NW)]
    CSTB = alloc("CST", BF16, CST_COLS)
    PPB = alloc("PP", F32, PP_COLS + 60)
    PFB = alloc("PF", F32, PF_COLS)
    BD = alloc("BD", BF16, 8 * 128)
    RG = alloc("RG", BF16, 4 * 128)
    scratch0 = cur[0]
    SCR_BYTES = ARENA_BYTES - scratch0

    def scr_alloc_reset():
        cur[0] = scratch0

    Xv = X.ap.rearrange("p (c t) -> p c t", c=8)
    XNv = XN.ap.rearrange("p (c t) -> p c t", c=8)
    Yv = Y.ap.rearrange("p (c t) -> p c t", c=8)
    cst = CSTB.ap
    Ctab = cst[:, C_OFF:C_OFF + S]
    Stab = cst[:, S_OFF:S_OFF + S]
    ones_bf = cst[:, ONES_OFF:ONES_OFF + 128]
    bones_bf = cst[:, BONES_OFF:BONES_OFF + 128]
    ident_bf = cst[:, IDENT_OFF:IDENT_OFF + 128]
    pp = PPB.ap
    pf = PFB.ap
    DV = PP_COLS
    C_EPS, C_NSP, C_NSP2, C_NLAM, C_TMP, C_SUBG = DV, DV + 1, DV + 5, DV + 9, DV + 10, DV + 12

    ps_all = nc.alloc_psum_tensor("ps", [128, 8, 512], F32).ap()

    def PS(b):
        return ps_all[:, b, :]

    def PSR(b):
        return [f"ps{b}"]

    def xres(c, tt):
        return X.res(c * S + tt * TT, c * S + (tt + 1) * TT)

    def xnres(c, tt):
        return XN.res(c * S + tt * TT, c * S + (tt + 1) * TT)

    def yres(c, lo, hi):
        return Y.res(c * S + lo, c * S + hi)

    for i in range(0, CST_COLS, 1024):
        j = min(i + 1024, CST_COLS)
        P.op("pool", lambda e, i=i, j=j: e.dma_start(out=cst[:, i:j], in_=cst_d[:, i:j]),
             writes=CSTB.res(i, j), lane="cst")
    P.op("sp", lambda e: e.dma_start(out=pp[:, 0:PP_COLS], in_=pp_d), writes=PPB.res(), lane="io0")
    P.op("sp", lambda e: e.dma_start(out=pf, in_=pf_d), writes=PFB.res(), lane="io1")
    for c4 in range(4):
        P.op("pool", lambda e, c4=c4: e.dma_start(out=BD.ap[:, c4 * 128:(c4 + 1) * 128], in_=wa_d[c4]),
             writes=BD.res(), lane="cst")
        P.op("pool", lambda e, c4=c4: e.dma_start(out=BD.ap[:, (4 + c4) * 128:(5 + c4) * 128], in_=wx_d[c4]),
             writes=BD.res(), lane="cst")
    P.op("dve", lambda e: e.memset(pp[:, C_EPS:C_EPS + 1], EPS), writes=PPB.res())
    P.op("act", lambda e: e.activation(out=pp[:, C_NSP:C_NSP + 4], in_=pp[:, PP_LAM:PP_LAM + 4], func=AF.Exp, scale=-1.0),
         reads=PPB.res(), writes=PPB.res())
    P.op("act", lambda e: e.activation(out=pp[:, C_NSP:C_NSP + 4], in_=pp[:, C_NSP:C_NSP + 4], func=AF.Ln, bias=1.0),
         reads=PPB.res(), writes=PPB.res())
    P.op("dve", lambda e: e.tensor_scalar(out=pp[:, C_NSP2:C_NSP2 + 4], in0=pp[:, C_NSP:C_NSP + 4], scalar1=-16.0, scalar2=None, op0=ALU.mult),
         reads=PPB.res(), writes=PPB.res())
    P.op("dve", lambda e: e.tensor_scalar(out=pp[:, C_NSP:C_NSP + 4], in0=pp[:, C_NSP:C_NSP + 4], scalar1=-8.0, scalar2=None, op0=ALU.mult),
         reads=PPB.res(), writes=PPB.res())
    LAMBDA_INIT0 = 0.8 - 0.6 * math.exp(-0.3 * 0)
    for i in range(2):
        a0 = 128 + 128 * i
        P.op("dve", lambda e, a0=a0: e.tensor_tensor(out=pf[:, a0:a0 + 64], in0=pf[:, a0:a0 + 64], in1=pf[:, a0 + 64:a0 + 128], op=ALU.mult),
             reads=PFB.res(), writes=PFB.res())
        P.op("dve", lambda e, a0=a0, i=i: e.tensor_reduce(out=pp[:, C_TMP + i:C_TMP + i + 1], in_=pf[:, a0:a0 + 64], axis=AX.X, op=ALU.add),
             reads=PFB.res(), writes=PPB.res())
    P.op("act", lambda e: e.activation(out=pp[:, C_TMP:C_TMP + 2], in_=pp[:, C_TMP:C_TMP + 2], func=AF.Exp),
         reads=PPB.res(), writes=PPB.res())
    P.op("dve", lambda e: e.tensor_tensor(out=pp[:, C_NLAM:C_NLAM + 1], in0=pp[:, C_TMP + 1:C_TMP + 2], in1=pp[:, C_TMP:C_TMP + 1], op=ALU.subtract),
         reads=PPB.res(), writes=PPB.res())
    P.op("dve", lambda e: e.tensor_scalar(out=pp[:, C_NLAM:C_NLAM + 1], in0=pp[:, C_NLAM:C_NLAM + 1], scalar1=-LAMBDA_INIT0, scalar2=None, op0=ALU.add),
         reads=PPB.res(), writes=PPB.res())
    P.op("dve", lambda e: e.tensor_scalar(out=pp[:, C_SUBG:C_SUBG + 1], in0=pp[:, PP_SUBG:PP_SUBG + 1], scalar1=1.0 - LAMBDA_INIT0, scalar2=None, op0=ALU.mult),
         reads=PPB.res(), writes=PPB.res())
    P.op("dve", lambda e: e.tensor_scalar(out=pf[:, 0:128], in0=pf[:, 0:128], scalar1=1.0 - LAMBDA_INIT0, scalar2=None, op0=ALU.mult),
         reads=PFB.res(), writes=PFB.res())
    for i in range(4):
        P.op("dve", lambda e, i=i: e.tensor_scalar(out=RG.ap[:, i * 128:(i + 1) * 128], in0=cst[:, R_OFF:R_OFF + 128],
                                                    scalar1=pp[:, PP_QKG + i:PP_QKG + i + 1], scalar2=None, op0=ALU.mult),
             reads=CSTB.res() + PPB.res(), writes=RG.res())

    wcnt = [0]

    def load_w(src):
        slot = wcnt[0] % NW
        wcnt[0] += 1
        P.op("pool", lambda e, slot=slot, src=src: e.dma_start(out=WR[slot].ap, in_=src),
             writes=WR[slot].res(), lane=f"w{slot}")
        return WR[slot]

    pcnt = [0]

    def next_bank(lo=0, n=8):
        b = lo + pcnt[0] % n
        pcnt[0] += 1
        return b

    dbg_names = []

    def dump_buf(name, ap, reads):
        if name in dump_out:
            n = ap.shape[1]
            for i in range(0, n, 1024):
                P.op("pool", lambda e, i=i: e.dma_start(out=dump_out[name][:, i:i + 1024], in_=ap[:, i:i + 1024]), reads=reads,
                     writes=["dbg_" + name + str(i)], lane="dbg")
                dbg_names.append("dbg_" + name + str(i))

    def rmsnorm(norm_idx, T0):
        SQ = [alloc(f"nsq{i}", BF16, TT) for i in range(2)]
        RS = [alloc(f"nrs{i}", F32, TT) for i in range(2)]
        for tt in range(NTT):
            b = next_bank()
            ts = slice(tt * TT, (tt + 1) * TT)
            for c in range(8):
                sq = SQ[c % 2]
                P.op("act", lambda e, sq=sq, c=c, ts=ts: e.activation(out=sq.ap, in_=Xv[:, c, ts], func=AF.Square),
                     reads=xres(c, tt), writes=sq.res())
                P.op("pe", lambda e, sq=sq, c=c, b=b: e.matmul(PS(b), lhsT=ones_bf, rhs=sq.ap, start=(c == 0), stop=(c == 7)),
                     reads=sq.res() + CSTB.res(ONES_OFF, ONES_OFF + 128), writes=PSR(b))
            rs = RS[tt % 2]
            P.op("act", lambda e, rs=rs, b=b: e.activation(out=rs.ap, in_=PS(b), func=AF.Ln, scale=1.0 / D, bias=pp[:, C_EPS:C_EPS + 1]),
                 reads=PSR(b) + PPB.res(), writes=rs.res())
            P.op("act", lambda e, rs=rs: e.activation(out=rs.ap, in_=rs.ap, func=AF.Exp, scale=-0.5),
                 reads=rs.res(), writes=rs.res())
            for c in range(8):
                gcol = PP_NORM + 8 * norm_idx + c
                P.op("dve", lambda e, rs=rs, c=c, ts=ts, gcol=gcol: e.scalar_tensor_tensor(
                    out=XNv[:, c, ts], in0=Xv[:, c, ts], scalar=pp[:, gcol:gcol + 1], in1=rs.ap, op0=ALU.mult, op1=ALU.mult),
                     reads=xres(c, tt) + rs.res() + PPB.res(), writes=xnres(c, tt))

    def proj_fm(wb, tt, b):
        ts = slice(tt * TT, (tt + 1) * TT)
        wv = wb.ap.rearrange("p (c n) -> p c n", c=8)

        def fn(e):
            ins = None
            for c in range(8):
                ins = e.matmul(PS(b), lhsT=wv[:, c, :], rhs=XNv[:, c, ts], start=(c == 0), stop=(c == 7))
            return ins
        P.op("pe", fn, reads=wb.res() + [r for c in range(8) for r in xnres(c, tt)], writes=PSR(b))

    def qk_post(b, tt, rg_idx, dst, T):
        ts = slice(tt * TT, (tt + 1) * TT)
        sq, qb, rs, t1, t2 = T
        gcol = PP_QKG + rg_idx
        P.op("act", lambda e: e.activation(out=sq.ap, in_=PS(b), func=AF.Square), reads=PSR(b), writes=sq.res())
        P.op("act", lambda e: e.activation(out=qb.ap, in_=PS(b), func=AF.Copy, scale=pp[:, gcol:gcol + 1]),
             reads=PSR(b) + PPB.res(), writes=qb.res())
        b1 = next_bank()
        P.op("pe", lambda e: e.matmul(PS(b1), lhsT=bones_bf, rhs=sq.ap, start=True, stop=True),
             reads=sq.res() + CSTB.res(BONES_OFF, BONES_OFF + 128), writes=PSR(b1))
        b2 = next_bank()
        P.op("pe", lambda e: e.matmul(PS(b2), lhsT=cst[:, R_OFF:R_OFF + 128], rhs=qb.ap, start=True, stop=True),
             reads=qb.res() + CSTB.res(R_OFF, R_OFF + 128), writes=PSR(b2))
        P.op("act", lambda e: e.activation(out=rs.ap, in_=PS(b1), func=AF.Ln, scale=1.0 / 64, bias=pp[:, C_EPS:C_EPS + 1]),
             reads=PSR(b1) + PPB.res(), writes=rs.res())
        P.op("act", lambda e: e.activation(out=rs.ap, in_=rs.ap, func=AF.Exp, scale=-0.5), reads=rs.res(), writes=rs.res())
        P.op("act", lambda e: e.activation(out=t2.ap, in_=PS(b2), func=AF.Copy), reads=PSR(b2), writes=t2.res())
        P.op("dve", lambda e: e.tensor_tensor(out=t1.ap, in0=qb.ap, in1=Ctab[:, ts], op=ALU.mult),
             reads=qb.res() + CSTB.res(C_OFF + tt * TT, C_OFF + (tt + 1) * TT), writes=t1.res())
        P.op("dve", lambda e: e.tensor_tensor(out=t2.ap, in0=t2.ap, in1=Stab[:, ts], op=ALU.mult),
             reads=t2.res() + CSTB.res(S_OFF + tt * TT, S_OFF + (tt + 1) * TT), writes=t2.res())
        P.op("dve", lambda e: e.tensor_tensor(out=t1.ap, in0=t1.ap, in1=t2.ap, op=ALU.add),
             reads=t1.res() + t2.res(), writes=t1.res())
        if isinstance(dst, tuple):
            for hf, dd in enumerate(dst):
                pr = slice(64 * hf, 64 * hf + 64)
                P.op("dve", lambda e, pr=pr, dd=dd: e.tensor_tensor(out=dd.ap[pr, ts], in0=t1.ap[pr, :], in1=rs.ap[pr, :], op=ALU.mult),
                     reads=t1.res() + rs.res(), writes=dd.res(tt * TT, (tt + 1) * TT))
        else:
            P.op("dve", lambda e: e.tensor_tensor(out=dst.ap[:, ts], in0=t1.ap, in1=rs.ap, op=ALU.mult),
                 reads=t1.res() + rs.res(), writes=dst.res(tt * TT, (tt + 1) * TT))

    def qk_post_pair(tt, wq, wk, rgq, rgk, dstq, dstk, Tq, Tk):
        ts = slice(tt * TT, (tt + 1) * TT)
        bq = next_bank()
        proj_fm(wq, tt, bq)
        bk = next_bank()
        proj_fm(wk, tt, bk)
        items = [(bq, rgq, dstq, Tq), (bk, rgk, dstk, Tk)]
        for (b, rg, dst, T) in items:
            sq, qb, rs, t1, t2 = T
            gcol = PP_QKG + rg
            P.op("act", lambda e, sq=sq, b=b: e.activation(out=sq.ap, in_=PS(b), func=AF.Square), reads=PSR(b), writes=sq.res())
            P.op("act", lambda e, qb=qb, b=b, gcol=gcol: e.activation(out=qb.ap, in_=PS(b), func=AF.Copy, scale=pp[:, gcol:gcol + 1]),
                 reads=PSR(b) + PPB.res(), writes=qb.res())
        bb = []
        for (b, rg, dst, T) in items:
            sq, qb, rs, t1, t2 = T
            b1 = next_bank()
            P.op("pe", lambda e, sq=sq, b1=b1: e.matmul(PS(b1), lhsT=bones_bf, rhs=sq.ap, start=True, stop=True),
                 reads=sq.res() + CSTB.res(BONES_OFF, BONES_OFF + 128), writes=PSR(b1))
            b2 = next_bank()
            P.op("pe", lambda e, qb=qb, b2=b2: e.matmul(PS(b2), lhsT=cst[:, R_OFF:R_OFF + 128], rhs=qb.ap, start=True, stop=True),
                 reads=qb.res() + CSTB.res(R_OFF, R_OFF + 128), writes=PSR(b2))
            bb.append((b1, b2))
        for (b, rg, dst, T), (b1, b2) in zip(items, bb):
            sq, qb, rs, t1, t2 = T
            P.op("act", lambda e, rs=rs, b1=b1: e.activation(out=rs.ap, in_=PS(b1), func=AF.Ln, scale=1.0 / 64, bias=pp[:, C_EPS:C_EPS + 1]),
                 reads=PSR(b1) + PPB.res(), writes=rs.res())
            P.op("act", lambda e, rs=rs: e.activation(out=rs.ap, in_=rs.ap, func=AF.Exp, scale=-0.5), reads=rs.res(), writes=rs.res())
            P.op("act", lambda e, t2=t2, b2=b2: e.activation(out=t2.ap, in_=PS(b2), func=AF.Copy), reads=PSR(b2), writes=t2.res())
        for (b, rg, dst, T) in items:
            sq, qb, rs, t1, t2 = T
            P.op("dve", lambda e, t1=t1, qb=qb: e.tensor_tensor(out=t1.ap, in0=qb.ap, in1=Ctab[:, ts], op=ALU.mult),
                 reads=qb.res() + CSTB.res(C_OFF + tt * TT, C_OFF + (tt + 1) * TT), writes=t1.res())
            P.op("dve", lambda e, t2=t2: e.tensor_tensor(out=t2.ap, in0=t2.ap, in1=Stab[:, ts], op=ALU.mult),
                 reads=t2.res() + CSTB.res(S_OFF + tt * TT, S_OFF + (tt + 1) * TT), writes=t2.res())
            P.op("dve", lambda e, t1=t1, t2=t2: e.tensor_tensor(out=t1.ap, in0=t1.ap, in1=t2.ap, op=ALU.add),
                 reads=t1.res() + t2.res(), writes=t1.res())
            if isinstance(dst, tuple):
                for hf, dd in enumerate(dst):
                    pr = slice(64 * hf, 64 * hf + 64)
                    P.op("dve", lambda e, pr=pr, dd=dd, t1=t1, rs=rs: e.tensor_tensor(out=dd.ap[pr, ts], in0=t1.ap[pr, :], in1=rs.ap[pr, :], op=ALU.mult),
                         reads=t1.res() + rs.res(), writes=dd.res(tt * TT, (tt + 1) * TT))
            else:
                P.op("dve", lambda e, dst=dst, t1=t1, rs=rs: e.tensor_tensor(out=dst.ap[:, ts], in0=t1.ap, in1=rs.ap, op=ALU.mult),
                     reads=t1.res() + rs.res(), writes=dst.res(tt * TT, (tt + 1) * TT))

    def v_proj(wb, VA, vw, col0, ncols, segs):
        wv = wb.ap.rearrange("p (c n) -> p c n", c=8)
        VAv = VA.ap.rearrange("p (k w) -> p k w", w=vw)
        for g4 in range(NQB // 4):
            b = next_bank()

            def fn(e, g4=g4, b=b):
                ins = None
                for i in range(4):
                    blk = g4 * 4 + i
                    for c in range(8):
                        ins = e.matmul(PS(b)[:, i * 128:(i + 1) * 128], lhsT=XNv[:, c, blk * 128:(blk + 1) * 128], rhs=wv[:, c, :],
                                       start=(c == 0), stop=(c == 7))
                return ins
            P.op("pe", fn, reads=wb.res() + [r for c in range(8) for r in xnres(c, g4)], writes=PSR(b))
            psv = PS(b).rearrange("p (i n) -> p i n", i=4)
            for (slo, shi, dlo) in segs:
                P.op("act", lambda e, g4=g4, psv=psv, slo=slo, shi=shi, dlo=dlo: e.activation(
                    out=VAv[:, g4 * 4:(g4 + 1) * 4, dlo:dlo + (shi - slo)], in_=psv[:, :, slo:shi], func=AF.Copy),
                     reads=PSR(b), writes=VA.res(g4 * 4 * vw, (g4 + 1) * 4 * vw))

    def attention(QT, KT, VA, vw, dv, ET, units, mask_fn):
        VAv = VA.ap.rearrange("p (k w) -> p k w", w=vw)
        sbanks = [0, 1, 2]
        obanks = [4, 5, 6, 7]
        stages = []
        for (Q, prow, vcol, fin) in units:
            for j in range(4 * Q + 4):
                stages.append((Q, prow, vcol, fin, j))

        def stage_a(n):
            Q, prow, vcol, fin, j = stages[n]
            r = max(0, j - 4 * Q)
            lo = r * 128
            sb = sbanks[n % 3]
            et = ET[n % len(ET)]
            QZ = prow
            lm = mask_fn(et, Q, j, r)

            def fn(e):
                ins = e.matmul(PS(sb)[:, lo:TT], lhsT=KT.ap[:, j * 128:(j + 1) * 128],
                               rhs=QZ.ap[:, Q * TT + lo:(Q + 1) * TT], start=True, stop=(lm is None))
                if lm is not None:
                    plo, ncol, tlo = lm
                    ins = e.matmul(PS(sb)[:, plo:plo + ncol], lhsT=ident_bf, rhs=cst[:, tlo:tlo + ncol], start=False, stop=True)
                return ins
            P.op("pe", fn, reads=KT.res(j * 128, (j + 1) * 128) + QZ.res(Q * TT + lo, (Q + 1) * TT) + CSTB.res(ML1_OFF, IDENT_OFF + 128),
                 writes=PSR(sb))
            P.op("act", lambda e: e.activation(out=et.ap[:, lo:TT], in_=PS(sb)[:, lo:TT], func=AF.Exp, scale=0.125),
                 reads=PSR(sb), writes=et.res())

        def stage_b(n):
            Q, prow, vcol, fin, j = stages[n]
            r = max(0, j - 4 * Q)
            et = ET[n % len(ET)]
            for il in range(r, 4):
                i = 4 * Q + il
                ob = obanks[il]
                P.op("pe", lambda e, il=il, i=i, ob=ob: e.matmul(
                    PS(ob)[:, 0:dv + 1], lhsT=et.ap[:, il * 128:(il + 1) * 128], rhs=VAv[:, j, vcol:vcol + dv + 1],
                    start=(j == 0), stop=(j == i)),
                     reads=et.res() + VA.res(j * vw, (j + 1) * vw), writes=PSR(ob))
                if j == i:
                    fin(i, ob)

        LOOK = 4
        for n in range(min(LOOK, len(stages))):
            stage_a(n)
        for n in range(len(stages)):
            stage_b(n)
            if n + LOOK < len(stages):
                stage_a(n + LOOK)

    def stop(tag):
        if stop_after == tag:
            raise _Stop()

    def run_seq(s):
        for c in range(8):
            P.op("sp", lambda e, c=c, s=s: e.dma_start(out=Xv[:, c, :], in_=xT[s, c * 128:(c + 1) * 128, :]),
                 writes=X.res(c * S, (c + 1) * S), lane=f"io{c % 2}")

        scr_alloc_reset()
        rmsnorm(0, None)
        if s == 0:
            dump_buf("xn0", XN.ap, XN.res())
        scr_alloc_reset()
        XR = alloc("XR", F32, S + 4)
        U = alloc("U", F32, S)
        RB = alloc("RB", F32, S)
        IB = alloc("IB", F32, S)
        GG = alloc("GG", BF16, S)
        UB = alloc("UB", BF16, S)
        P.op("dve", lambda e: e.memset(XR.ap[:, 0:4], 0.0), writes=XR.res(0, 4))
        for c4 in range(4):
            wxr = load_w(w_in[c4])
            wgt = load_w(w_in[4 + c4])
            for tt in range(NTT):
                ts = slice(tt * TT, (tt + 1) * TT)
                b = next_bank()
                proj_fm(wxr, tt, b)
                P.op("act", lambda e, b=b, tt=tt: e.activation(out=XR.ap[:, 4 + tt * TT:4 + (tt + 1) * TT], in_=PS(b), func=AF.Copy),
                     reads=PSR(b), writes=XR.res(4 + tt * TT, 4 + (tt + 1) * TT))
                b = next_bank()
                proj_fm(wgt, tt, b)
                P.op("act", lambda e, b=b, ts=ts: e.activation(out=GG.ap[:, ts], in_=PS(b), func=AF.Gelu_apprx_tanh),
                     reads=PSR(b), writes=GG.res(tt * TT, (tt + 1) * TT))
            cw = PP_CONVW + 4 * c4
            P.op("dve", lambda e, cw=cw, c4=c4: e.tensor_scalar(out=U.ap, in0=XR.ap[:, 4:4 + S], scalar1=pp[:, cw + 3:cw + 4],
                                                                 scalar2=pp[:, PP_CONVB + c4:PP_CONVB + c4 + 1], op0=ALU.mult, op1=ALU.add),
                 reads=XR.res() + PPB.res(), writes=U.res())
            for j in range(3):
                P.op("dve", lambda e, cw=cw, j=j: e.scalar_tensor_tensor(out=U.ap, in0=XR.ap[:, 1 + j:1 + j + S], scalar=pp[:, cw + j:cw + j + 1],
                                                                         in1=U.ap, op0=ALU.mult, op1=ALU.add),
                     reads=XR.res() + U.res() + PPB.res(), writes=U.res())
            P.op("dve", lambda e: e.tensor_copy(out=UB.ap, in_=U.ap), reads=U.res(), writes=UB.res())
            for tt in range(NTT):
                ts = slice(tt * TT, (tt + 1) * TT)
                for (gi, dstb, bcol) in ((0, RB, PP_BA), (1, IB, PP_BX)):
                    b = next_bank()
                    P.op("pe", lambda e, b=b, gi=gi, c4=c4, ts=ts: e.matmul(PS(b), lhsT=BD.ap[:, (4 * gi + c4) * 128:(4 * gi + c4 + 1) * 128],
                                                                             rhs=UB.ap[:, ts], start=True, stop=True),
                         reads=BD.res() + UB.res(tt * TT, (tt + 1) * TT), writes=PSR(b))
                    P.op("act", lambda e, b=b, dstb=dstb, ts=ts, bcol=bcol, c4=c4: e.activation(
                        out=dstb.ap[:, ts], in_=PS(b), func=AF.Sigmoid, bias=pp[:, bcol + c4:bcol + c4 + 1]),
                         reads=PSR(b) + PPB.res(), writes=dstb.res(tt * TT, (tt + 1) * TT))
            Tap = XR.ap[:, 4:4 + S]
            P.op("act", lambda e, c4=c4: e.activation(out=Tap, in_=RB.ap, func=AF.Exp, scale=pp[:, C_NSP2 + c4:C_NSP2 + c4 + 1]),
                 reads=RB.res() + PPB.res(), writes=XR.res())
            P.op("act", lambda e, c4=c4: e.activation(out=RB.ap, in_=RB.ap, func=AF.Exp, scale=pp[:, C_NSP + c4:C_NSP + c4 + 1]),
                 reads=RB.res() + PPB.res(), writes=RB.res())
            P.op("act", lambda e: e.activation(out=Tap, in_=Tap, func=AF.Sqrt, scale=-1.0, bias=1.0), reads=XR.res(), writes=XR.res())
            P.op("dve", lambda e: e.tensor_tensor(out=IB.ap, in0=IB.ap, in1=U.ap, op=ALU.mult), reads=IB.res() + U.res(), writes=IB.res())
            P.op("dve", lambda e: e.tensor_tensor(out=IB.ap, in0=IB.ap, in1=Tap, op=ALU.mult), reads=IB.res() + XR.res(), writes=IB.res())
            P.op("dve", lambda e: e.tensor_tensor_scan(out=U.ap, data0=RB.ap, data1=IB.ap, initial=0.0, op0=ALU.mult, op1=ALU.add),
                 reads=RB.res() + IB.res(), writes=U.res())
            P.op("dve", lambda e, c4=c4: e.tensor_tensor(out=Yv[:, c4, :], in0=U.ap, in1=GG.ap, op=ALU.mult),
                 reads=U.res() + GG.res(), writes=yres(c4, 0, S))
        if s == 0:
            dump_buf("y0", Y.ap, Y.res())
        stop("rnn")
        scr_alloc_reset()
        QT = alloc("QT", BF16, S)
        QZ1 = alloc("QZ1", BF16, S)
        KT = alloc("KT", BF16, S)
        VW0 = 132
        VA = alloc("VA", BF16, NQB * VW0)
        ET = [alloc(f"ET{i}", BF16, TT) for i in range(5)]
        P.op("dve", lambda e: e.memset(QT.ap[64:128, :], 0.0), writes=QT.res())
        P.op("dve", lambda e: e.memset(QZ1.ap[0:64, :], 0.0), writes=QZ1.res())
        TQ = (alloc("tsq", BF16, TT), alloc("tqb", BF16, TT), alloc("trs", F32, TT), alloc("tt1", F32, TT), alloc("tt2", F32, TT))
        TQb = (alloc("tsqb", BF16, TT), alloc("tqbb", BF16, TT), alloc("trsb", F32, TT), alloc("tt1b", F32, TT), alloc("tt2b", F32, TT))
        OST = alloc("OST", F32, 132)
        OS = alloc("OS", BF16, 4 * 128)
        SM = alloc("SM", F32, 8)
        OAS = alloc("OAS", F32, 4 * 128)
        VAv0 = VA.ap.rearrange("p (k w) -> p k w", w=VW0)
        nxt0 = (load_w(w_in[8]), load_w(w_in[12]), load_w(w_in[16]))
        for h in range(4):
            wq, wk, wvv = nxt0
            if h < 3:
                nxt0 = (load_w(w_in[8 + h + 1]), load_w(w_in[12 + h + 1]), load_w(w_in[16 + h + 1]))
            for tt in range(NTT):
                qk_post_pair(tt, wq, wk, 0, 1, (QT, QZ1), KT, TQ, TQb)
            if s == 0 and h == 0:
                dump_buf("qt0", QT.ap, QT.res())
                dump_buf("kt0", KT.ap, KT.res())
            stop("qk")
            v_proj(wvv, VA, VW0, 0, 128, [(0, 128, 0)])
            P.op("dve", lambda e: e.memset(VAv0[:, :, 128:129], 1.0), writes=VA.res())
            stop("vproj")

            def mask0(et, Q, j, r):
                if j >= 4 * Q:
                    return (r * 128, 128, MCUR_OFF)
                return None

            def fin_b(i, ob, h=h):
                il = i % 4
                P.op("act", lambda e, ob=ob: e.activation(out=OST.ap[:, 0:129], in_=PS(ob)[:, 0:129], func=AF.Copy), reads=PSR(ob), writes=OST.res())
                P.op("dve", lambda e, ob=ob: e.reciprocal(out=SM.ap[:, 1:2], in_=OST.ap[:, 128:129]), reads=OST.res(), writes=SM.res())
                P.op("dve", lambda e: e.tensor_tensor(out=SM.ap[:, 1:2], in0=SM.ap[:, 1:2], in1=pp[:, C_NLAM:C_NLAM + 1], op=ALU.mult),
                     reads=SM.res() + PPB.res(), writes=SM.res())
                P.op("dve", lambda e, ob=ob, i=i, il=il: e.scalar_tensor_tensor(out=OS.ap[:, il * 128:(il + 1) * 128], in0=OST.ap[:, 0:128], scalar=SM.ap[:, 1:2],
                                                                                in1=OAS.ap[:, il * 128:(il + 1) * 128], op0=ALU.mult, op1=ALU.add),
                     reads=OST.res() + SM.res() + OAS.res(il * 128, (il + 1) * 128), writes=OS.res())
                if il == 3:
                    Q = i // 4
                    tb = 3
                    psb = PS(tb).bitcast(BF16)
                    sq, qb, rs, t1, t2 = TQ

                    def fn(e):
                        ins = None
                        for k in range(4):
                            ins = e.transpose(psb[:, k * 128:(k + 1) * 128], OS.ap[:, k * 128:(k + 1) * 128], ident_bf)
                        return ins
                    P.op("pe", fn, reads=OS.res() + CSTB.res(IDENT_OFF, IDENT_OFF + 128), writes=PSR(tb))
                    P.op("act", lambda e: e.activation(out=t1.ap, in_=psb[:, 0:TT], func=AF.Copy), reads=PSR(tb), writes=t1.res())
                    P.op("act", lambda e: e.activation(out=sq.ap, in_=psb[:, 0:TT], func=AF.Square), reads=PSR(tb), writes=sq.res())
                    P.op("pe", lambda e: e.matmul(PS(tb), lhsT=ones_bf, rhs=sq.ap, start=True, stop=True),
                         reads=sq.res() + CSTB.res(ONES_OFF, ONES_OFF + 128), writes=PSR(tb))
                    P.op("act", lambda e: e.activation(out=rs.ap, in_=PS(tb), func=AF.Ln, scale=1.0 / 128, bias=pp[:, C_EPS:C_EPS + 1]),
                         reads=PSR(tb) + PPB.res(), writes=rs.res())
                    P.op("act", lambda e: e.activation(out=rs.ap, in_=rs.ap, func=AF.Exp, scale=-0.5), reads=rs.res(), writes=rs.res())
                    P.op("dve", lambda e, Q=Q, h=h: e.scalar_tensor_tensor(out=Yv[:, 4 + h, Q * TT:(Q + 1) * TT], in0=t1.ap, scalar=pp[:, C_SUBG:C_SUBG + 1],
                                                                            in1=rs.ap, op0=ALU.mult, op1=ALU.mult),
                         reads=t1.res() + rs.res() + PPB.res(), writes=yres(4 + h, Q * TT, (Q + 1) * TT))

            def fin_a2(i, ob):
                P.op("act", lambda e, ob=ob: e.activation(out=OST.ap[:, 0:129], in_=PS(ob)[:, 0:129], func=AF.Copy), reads=PSR(ob), writes=OST.res())
                P.op("dve", lambda e, ob=ob: e.reciprocal(out=SM.ap[:, 0:1], in_=OST.ap[:, 128:129]), reads=OST.res(), writes=SM.res())
                P.op("dve", lambda e, ob=ob, i=i: e.tensor_scalar(out=OAS.ap[:, (i % 4) * 128:(i % 4 + 1) * 128], in0=OST.ap[:, 0:128], scalar1=SM.ap[:, 0:1],
                                                                   scalar2=None, op0=ALU.mult),
                     reads=OST.res() + SM.res(), writes=OAS.res((i % 4) * 128, (i % 4 + 1) * 128))

            units = [(Q, (QT, QZ1)[c], 0, (fin_a2, fin_b)[c]) for Q in range(NTT) for c in range(2)]
            if stop_after == "att_a":
                attention(QT, KT, VA, VW0, 128, ET, units[:NTT], mask0)
                stop("att_a")
            attention(QT, KT, VA, VW0, 128, ET, units, mask0)
            stop("att_b")
        if s == 0:
            dump_buf("y0b", Y.ap, Y.res())
        stop("attn0")

        def out_proj(wdram):
            for oc in range(8):
                wb = load_w(wdram[oc])
                wv = wb.ap.rearrange("p (c n) -> p c n", c=8)
                for tt in range(NTT):
                    ts = slice(tt * TT, (tt + 1) * TT)
                    b = next_bank()

                    def fn(e, wv=wv, ts=ts, b=b):
                        ins = None
                        for c in range(8):
                            ins = e.matmul(PS(b), lhsT=wv[:, c, :], rhs=Yv[:, c, ts], start=(c == 0), stop=(c == 7))
                        return ins
                    P.op("pe", fn, reads=wb.res() + [r for c in range(8) for r in yres(c, tt * TT, (tt + 1) * TT)], writes=PSR(b))
                    ev = EV[evc[0] % 2]
                    evc[0] += 1
                    P.op("act", lambda e, b=b, ev=ev: e.activation(out=ev.ap, in_=PS(b), func=AF.Copy), reads=PSR(b), writes=ev.res())
                    P.op("dve", lambda e, oc=oc, ts=ts, ev=ev: e.tensor_tensor(out=Xv[:, oc, ts], in0=Xv[:, oc, ts], in1=ev.ap, op=ALU.add),
                         reads=ev.res() + xres(oc, tt), writes=xres(oc, tt))

        scr_alloc_reset()
        EV = [alloc(f"ev{i}", F32, TT) for i in range(2)]
        evc = [0]
        out_proj(w_out0)
        if s == 0:
            dump_buf("x1", X.ap, X.res())
        stop("mix0")

        def ffn(l):
            scr_alloc_reset()
            rmsnorm(1 + 2 * l, None)
            scr_alloc_reset()
            HQ = 4
            H = alloc("H", BF16, HQ * S)
            Hv = H.ap.rearrange("p (k t) -> p k t", k=HQ)
            SG = [alloc(f"sg{i}", F32, TT) for i in range(2)]
            SU = [alloc(f"su{i}", F32, TT) for i in range(2)]
            EVF = [alloc(f"evf{i}", F32, TT) for i in range(2)]
            groups = [list(range(g, min(g + HQ, NHB))) for g in range(0, NHB, HQ)]
            for grp in groups:
                for k, hb in enumerate(grp):
                    wg = load_w(w_gate[l][hb])
                    wu = load_w(w_up[l][hb])
                    for tt in range(NTT):
                        ts = slice(tt * TT, (tt + 1) * TT)
                        bg = next_bank()
                        proj_fm(wg, tt, bg)
                        bu = next_bank()
                        proj_fm(wu, tt, bu)
                        sg = SG[tt % 2]
                        P.op("act", lambda e, bg=bg, sg=sg: e.activation(out=sg.ap, in_=PS(bg), func=AF.Silu), reads=PSR(bg), writes=sg.res())
                        su = SU[tt % 2]
                        P.op("act", lambda e, bu=bu, su=su: e.activation(out=su.ap, in_=PS(bu), func=AF.Copy), reads=PSR(bu), writes=su.res())
                        P.op("dve", lambda e, su=su, sg=sg, k=k, ts=ts: e.tensor_tensor(out=Hv[:, k, ts], in0=sg.ap, in1=su.ap, op=ALU.mult),
                             reads=su.res() + sg.res(), writes=H.res(k * S + tt * TT, k * S + (tt + 1) * TT))
                wds = [load_w(w_down[l][hb]) for hb in grp]
                for oc in range(8):
                    for tt in range(NTT):
                        ts = slice(tt * TT, (tt + 1) * TT)
                        b = next_bank()

                        def fn(e, oc=oc, ts=ts, b=b, n=len(grp), wds=wds):
                            ins = None
                            for k in range(n):
                                ins = e.matmul(PS(b), lhsT=wds[k].ap[:, oc * 128:(oc + 1) * 128], rhs=Hv[:, k, ts], start=(k == 0), stop=(k == n - 1))
                            return ins
                        P.op("pe", fn, reads=[r for w in wds for r in w.res()] + [r for k in range(len(grp)) for r in H.res(k * S + tt * TT, k * S + (tt + 1) * TT)],
                             writes=PSR(b))
                        ev = EVF[(oc * NTT + tt) % 2]
                        P.op("act", lambda e, b=b, ev=ev: e.activation(out=ev.ap, in_=PS(b), func=AF.Copy), reads=PSR(b), writes=ev.res())
                        P.op("dve", lambda e, oc=oc, ts=ts, ev=ev: e.tensor_tensor(out=Xv[:, oc, ts], in0=Xv[:, oc, ts], in1=ev.ap, op=ALU.add),
                             reads=ev.res() + xres(oc, tt), writes=xres(oc, tt))

        ffn(0)
        if s == 0:
            dump_buf("x2", X.ap, X.res())
        stop("ffn0")

        scr_alloc_reset()
        rmsnorm(2, None)
        scr_alloc_reset()
        QT = alloc("QT1", BF16, S)
        QZ1 = alloc("QZ11", BF16, S)
        KT = alloc("KT1", BF16, S)
        VW1 = 132
        VA = alloc("VA1", BF16, NQB * VW1)
        ET = [alloc(f"ET1{i}", BF16, TT) for i in range(6)]
        P.op("dve", lambda e: e.memset(QT.ap[64:128, :], 0.0), writes=QT.res())
        P.op("dve", lambda e: e.memset(QZ1.ap[0:64, :], 0.0), writes=QZ1.res())
        TQ = (alloc("tsq1", BF16, TT), alloc("tqb1", BF16, TT), alloc("trs1", F32, TT), alloc("tt11", F32, TT), alloc("tt21", F32, TT))
        OS = alloc("OS1", BF16, 4 * 128)
        SM = alloc("SM1", F32, 8)
        OST = alloc("OST1", F32, 132)
        VAv1 = VA.ap.rearrange("p (k w) -> p k w", w=VW1)
        nxt1 = (load_w(w_qkv[0]), load_w(w_qkv[8]), load_w(w_qkv[16]))
        TQ2 = (alloc("tsq2", BF16, TT), alloc("tqb2", BF16, TT), alloc("trs2", F32, TT), alloc("tt12", F32, TT), alloc("tt22", F32, TT))
        for hp in range(8):
            wq, wk, wvv = nxt1
            if hp < 7:
                nxt1 = (load_w(w_qkv[hp + 1]), load_w(w_qkv[8 + hp + 1]), load_w(w_qkv[16 + hp + 1]))
            for tt in range(NTT):
                qk_post_pair(tt, wq, wk, 2, 3, (QT, QZ1), KT, TQ, TQ2)
            v_proj(wvv, VA, VW1, 0, 128, [(0, 64, 0), (64, 128, 65)])
            P.op("dve", lambda e: e.memset(VAv1[:, :, 64:65], 1.0), writes=VA.res())
            P.op("dve", lambda e: e.memset(VAv1[:, :, 129:130], 1.0), writes=VA.res())

            def mask1(et, Q, j, r):
                d0 = 4 * Q + r - j
                return (r * 128, (4 - r) * 128, ML1_OFF + d0 * 128)

            def make_fin(h2, hp=hp):
                def fin(i, ob):
                    il = i % 4
                    P.op("act", lambda e, ob=ob: e.activation(out=OST.ap[:, 0:65], in_=PS(ob)[:, 0:65], func=AF.Copy), reads=PSR(ob), writes=OST.res())
                    P.op("dve", lambda e, ob=ob: e.reciprocal(out=SM.ap[:, h2:h2 + 1], in_=OST.ap[:, 64:65]), reads=OST.res(), writes=SM.res())
                    P.op("dve", lambda e, ob=ob, il=il: e.tensor_scalar(out=OS.ap[:, il * 128 + h2 * 64:il * 128 + h2 * 64 + 64], in0=OST.ap[:, 0:64],
                                                                         scalar1=SM.ap[:, h2:h2 + 1], scalar2=None, op0=ALU.mult),
                         reads=OST.res() + SM.res(), writes=OS.res())
                    if il == 3 and h2 == 1:
                        Q = i // 4
                        tb = 3
                        psb = PS(tb).bitcast(BF16)

                        def fn(e):
                            ins = None
                            for k in range(4):
                                ins = e.transpose(psb[:, k * 128:(k + 1) * 128], OS.ap[:, k * 128:(k + 1) * 128], ident_bf)
                            return ins
                        P.op("pe", fn, reads=OS.res() + CSTB.res(IDENT_OFF, IDENT_OFF + 128), writes=PSR(tb))
                        P.op("act", lambda e, Q=Q: e.activation(out=Yv[:, hp, Q * TT:(Q + 1) * TT], in_=psb[:, 0:TT], func=AF.Copy),
                             reads=PSR(tb), writes=yres(hp, Q * TT, (Q + 1) * TT))
                return fin
            fins = [make_fin(0), make_fin(1)]
            units = [(Q, (QT, QZ1)[h2], 65 * h2, fins[h2]) for Q in range(NTT) for h2 in range(2)]
            attention(QT, KT, VA, VW1, 64, ET, units, mask1)
        if s == 0:
            dump_buf("y1", Y.ap, Y.res())
        stop("attn1")
        scr_alloc_reset()
        EV = [alloc(f"ev{i}", F32, TT) for i in range(2)]
        out_proj(w_out1)
        if s == 0:
            dump_buf("x3", X.ap, X.res())
        ffn(1)
        for c in range(8):
            P.op("sp", lambda e, c=c, s=s: e.dma_start(out=oT[s, c * 128:(c + 1) * 128, :], in_=Xv[:, c, :]),
                 reads=X.res(c * S, (c + 1) * S), writes=[f"out{s}_{c}"], lane=f"io{c % 2}")
    try:
        for s in range(nseq):
            run_seq(s)
    except _Stop:
        pass
    P.op("sp", lambda e: e.nop(), reads=[f"out{s}_{c}" for s in range(nseq) for c in range(8)] + dbg_names)
    P.emit()
    return nc, P


_CACHE = {}


def kernel(**inputs):
    x = np.asarray(inputs["x"], dtype=np.float32)
    sh = prep_shared(inputs)
    if "nc" not in _CACHE:
        _CACHE["nc"] = build_program(2)[0]
    nc = _CACHE["nc"]
    in_maps = []
    for c in range(NCORES):
        m = dict(sh)
        m["xT"] = np.ascontiguousarray(x[2 * c:2 * c + 2].transpose(0, 2, 1))
        in_maps.append(m)
    res = run_bass_kernel_spmd(nc, in_maps, core_ids=list(range(NCORES)))
    out = np.empty_like(x)
    for c in range(NCORES):
        out[2 * c:2 * c + 2] = np.asarray(res.results[c]["oT"]).transpose(0, 2, 1)
    return out
```

```python
import math
import contextlib
import numpy as np
import concourse.bass as bass
import concourse.mybir as mybir
from concourse.bass_utils import run_bass_kernel_spmd

F32 = mybir.dt.float32
BF16 = mybir.dt.bfloat16
AF = mybir.ActivationFunctionType
ALU = mybir.AluOpType
AX = mybir.AxisListType

NCORES = 8
S = 2048
D = 1024
TT = 512
NTT = S // TT
NQB = S // 128
DFF = 2816
NHB = DFF // 128
EPS = 1e-6
EPOCH = 1500
GR = 512


class Op:
    pass


class _Stop(Exception):
    pass


class Prog:
    ENGS = ("pe", "act", "dve", "pool", "sp")

    def __init__(self, nc):
        self.nc = nc
        self.ops = {e: [] for e in self.ENGS}
        self.lanes = {}
        self.last_w = {}
        self.readers = {}

    def _tok(self, o):
        return ("l", o.lane, o.lidx) if o.is_dma else ("e", o.eng, o.idx)

    def op(self, eng, fn, reads=(), writes=(), lane=None):
        o = Op()
        o.eng, o.fn, o.lane = eng, fn, lane
        o.is_dma = lane is not None
        o.deps, o.raw, o.signal, o.sigidx = set(), set(), False, None
        o.idx = len(self.ops[eng])
        self.ops[eng].append(o)
        if o.is_dma:
            L = self.lanes.setdefault(lane, [])
            o.lidx = len(L)
            if L:
                o.deps.add(self._tok(L[-1]))
            L.append(o)
        tok = self._tok(o)
        for r in reads:
            lw = self.last_w.get(r)
            if lw is not None:
                o.deps.add(lw)
                o.raw.add(lw)
        for w in writes:
            lw = self.last_w.get(w)
            if lw is not None:
                o.deps.add(lw)
            rl = self.readers.get(w)
            if rl:
                o.deps.update(rl)
        for r in reads:
            self.readers.setdefault(r, []).append(tok)
        for w in writes:
            self.last_w[w] = tok
            self.readers[w] = []
        o.deps.discard(tok)
        return o

    def emit(self):
        nc = self.nc
        known = {e: {} for e in self.ENGS}
        waits = {}
        for e in self.ENGS:
            kn = known[e]
            for o in self.ops[e]:
                need = {}
                for (kind, key, idx) in o.deps:
                    if kind == "e" and key == e and not o.is_dma:
                        if e in ("pe", "sp"):
                            continue
                        if (kind, key, idx) not in o.raw or o.idx - idx > 3:
                            continue
                    k = (kind, key)
                    if kn.get(k, -1) >= idx:
                        continue
                    if need.get(k, -1) < idx:
                        need[k] = idx
                wl = []
                for k, idx in need.items():
                    kn[k] = idx
                    wl.append((k[0], k[1], idx))
                    if k[0] == "e":
                        self.ops[k[1]][idx].signal = True
                waits[o] = wl
        nsig, cum = {}, {}
        for e in self.ENGS:
            c, arr = 0, []
            for o in self.ops[e]:
                if o.signal and not o.is_dma:
                    o.sigidx = c
                    c += 1
                arr.append(c)
            nsig[e], cum[e] = c, arr
        sems = {}
        st = contextlib.ExitStack()
        for e in self.ENGS:
            for ep in range(max((nsig[e] + EPOCH - 1) // EPOCH, 1)):
                sems[("e", e, ep)] = st.enter_context(nc.semaphore(f"s_{e}_{ep}"))
        for ln in self.lanes:
            sems[("l", ln)] = st.enter_context(nc.semaphore(f"l_{ln}"))
        self.n_sems = len(sems)
        block = st.enter_context(nc.Block())
        engmap = {"pe": block.tensor, "act": block.scalar, "dve": block.vector,
                  "pool": block.gpsimd, "sp": block.sync}

        def make_body(e):
            def body(eng):
                for o in self.ops[e]:
                    for (kind, key, idx) in waits[o]:
                        if kind == "e":
                            c = cum[key][idx]
                            ep = (c - 1) // EPOCH
                            eng.wait_ge(sems[("e", key, ep)], c - ep * EPOCH)
                        else:
                            eng.wait_ge(sems[("l", key)], 16 * (idx + 1))
                    ins = o.fn(eng)
                    if o.is_dma:
                        ins.then_inc(sems[("l", o.lane)], 16)
                    elif o.signal:
                        ins.then_inc(sems[("e", e, o.sigidx // EPOCH)], 1)
            return body

        for e in self.ENGS:
            if self.ops[e]:
                engmap[e](make_body(e))
        st.close()


class Buf:
    def __init__(self, arena_bf, off, dtype, cols, name):
        self.off, self.dtype, self.cols, self.name = off, dtype, cols, name
        self.esz = 4 if dtype == F32 else 2
        v = arena_bf[:, off // 2: off // 2 + (cols * self.esz) // 2]
        self.ap = v.bitcast(F32) if dtype == F32 else v

    def res(self, lo=0, hi=None):
        hi = self.cols if hi is None else hi
        b0 = self.off + lo * self.esz
        b1 = self.off + hi * self.esz
        return [f"g{g}" for g in range(b0 // GR, (b1 - 1) // GR + 1)]


CST_COLS = 2 * S + 16 * 128 + 5 * 128
C_OFF, S_OFF, ML1_OFF = 0, S, 2 * S
MCUR_OFF = ML1_OFF + 16 * 128
ONES_OFF = MCUR_OFF + 128
BONES_OFF = ONES_OFF + 128
IDENT_OFF = BONES_OFF + 128
R_OFF = IDENT_OFF + 128


def make_consts():
    cst = np.zeros((128, CST_COLS), np.float32)
    pos = np.arange(S, dtype=np.float32)
    inv_freq = (1.0 / (np.float32(500000.0) ** (np.arange(0, 16, 2, dtype=np.float32) / np.float32(16)))).astype(np.float32)
    ang = pos[None, :] * inv_freq[:, None]
    cos, sin = np.cos(ang).astype(np.float32), np.sin(ang).astype(np.float32)
    C = np.ones((128, S), np.float32)
    Sg = np.zeros((128, S), np.float32)
    R = np.zeros((128, 128), np.float32)
    for g in range(2):
        for d in range(8):
            C[64 * g + d] = cos[d]
            C[64 * g + 8 + d] = cos[d]
            Sg[64 * g + d] = -sin[d]
            Sg[64 * g + 8 + d] = sin[d]
            R[64 * g + 8 + d, 64 * g + d] = 1.0
            R[64 * g + d, 64 * g + 8 + d] = 1.0
    cst[:, C_OFF:C_OFF + S] = C
    cst[:, S_OFF:S_OFF + S] = Sg
    k = np.arange(128)[:, None]
    q = np.arange(128)[None, :]
    for dl in range(16):
        dist = 128 * dl + q - k
        m = ((dist >= 0) & (dist <= 128)).astype(np.float32)
        m += ((dist >= 0) & (dist <= 512) & (dist % 4 == 0)).astype(np.float32)
        m += ((dist >= 0) & (dist % 16 == 0)).astype(np.float32)
        lm = np.where(m > 0, 8.0 * np.log(np.maximum(m, 1.0)), -1000.0).astype(np.float32)
        cst[:, ML1_OFF + 128 * dl: ML1_OFF + 128 * (dl + 1)] = lm
    cst[:, MCUR_OFF:MCUR_OFF + 128] = np.where(q >= k, 0.0, -1000.0).astype(np.float32)
    cst[:, ONES_OFF:ONES_OFF + 128] = 1.0
    bo = np.zeros((128, 128), np.float32)
    bo[:64, :64] = 1.0
    bo[64:, 64:] = 1.0
    cst[:, BONES_OFF:BONES_OFF + 128] = bo
    cst[:, IDENT_OFF:IDENT_OFF + 128] = np.eye(128, dtype=np.float32)
    cst[:, R_OFF:R_OFF + 128] = R
    return cst


PP_NORM = 0
PP_CONVW = 32
PP_CONVB = 48
PP_BA = 52
PP_BX = 56
PP_LAM = 60
PP_QKG = 64
PP_SUBG = 68
PP_COLS = 69
PF_COLS = 128 + 4 * 64


def relayout_kn(W):
    K, N = W.shape
    return np.ascontiguousarray(W.reshape(K // 128, 128, N // 128, 128).transpose(2, 1, 0, 3)).reshape(N // 128, 128, (K // 128) * 128)


def prep_shared(inp):
    f = lambda a: np.asarray(a, dtype=np.float32)
    sh = {}
    sh["w_in"] = relayout_kn(f(inp["ab_w_in"])[0])
    sh["w_out0"] = relayout_kn(f(inp["ab_w_out"])[0])
    sh["w_qkv"] = relayout_kn(f(inp["c_w_qkv"])[0])
    sh["w_out1"] = relayout_kn(f(inp["c_w_out"])[0])
    for l in range(2):
        sh[f"w_gate{l}"] = relayout_kn(f(inp["ffn_w_gate"])[l])
        sh[f"w_up{l}"] = relayout_kn(f(inp["ffn_w_up"])[l])
        sh[f"w_down{l}"] = np.ascontiguousarray(f(inp["ffn_w_down"])[l].reshape(NHB, 128, D))
    pp = np.zeros((128, PP_COLS), np.float32)
    for i, g in enumerate([f(inp["ab_norm_g"])[0], f(inp["ffn_norm_g"])[0], f(inp["c_norm_g"])[0], f(inp["ffn_norm_g"])[1]]):
        pp[:, PP_NORM + 8 * i: PP_NORM + 8 * i + 8] = g.reshape(8, 128).T
    cw = f(inp["ab_conv_w"])[0]
    for c4 in range(4):
        pp[:, PP_CONVW + 4 * c4: PP_CONVW + 4 * c4 + 4] = cw[:, 128 * c4:128 * c4 + 128].T
    pp[:, PP_CONVB:PP_CONVB + 4] = f(inp["ab_conv_b"])[0].reshape(4, 128).T
    pp[:, PP_BA:PP_BA + 4] = f(inp["ab_ba"])[0].reshape(4, 128).T
    pp[:, PP_BX:PP_BX + 4] = f(inp["ab_bx"])[0].reshape(4, 128).T
    pp[:, PP_LAM:PP_LAM + 4] = f(inp["ab_lru_lambda"])[0].reshape(4, 128).T
    for i, nm in enumerate(["ab_q_norm_g", "ab_k_norm_g", "c_q_norm_g", "c_k_norm_g"]):
        pp[:, PP_QKG + i] = np.tile(f(inp[nm])[0], 2)
    pp[:, PP_SUBG] = f(inp["ab_subln_g"])[0]
    sh["pp"] = pp
    pf = np.zeros((128, PF_COLS), np.float32)
    pf[:, 0:128] = f(inp["ab_subln_g"])[0][None, :]
    for i, nm in enumerate(["ab_lambda_q1", "ab_lambda_k1", "ab_lambda_q2", "ab_lambda_k2"]):
        pf[:, 128 + 64 * i: 128 + 64 * (i + 1)] = f(inp[nm])[0][None, :]
    sh["pf"] = pf
    for nm, key in (("wa_bd", "ab_wa"), ("wx_bd", "ab_wx")):
        w = f(inp[key])[0]
        bd = np.zeros((4, 128, 128), np.float32)
        for c4 in range(4):
            bd[c4, :64, :64] = w[2 * c4]
            bd[c4, 64:, 64:] = w[2 * c4 + 1]
        sh[nm] = bd
    sh["cst"] = make_consts()
    return sh


def build_program(nseq=2, dump=None, stop_after=None):
    dump = dump or {}
    nc = bass.Bass("TRN2", target_bir_lowering=False)
    P = Prog(nc)

    def din(name, shape):
        return nc.dram_tensor(name, list(shape), F32, kind="ExternalInput").ap()

    xT = din("xT", [nseq, D, S])
    oT = nc.dram_tensor("oT", [nseq, D, S], F32, kind="ExternalOutput").ap()
    w_in = din("w_in", [20, 128, 1024])
    w_out0 = din("w_out0", [8, 128, 1024])
    w_qkv = din("w_qkv", [24, 128, 1024])
    w_out1 = din("w_out1", [8, 128, 1024])
    w_gate = [din(f"w_gate{l}", [NHB, 128, 1024]) for l in range(2)]
    w_up = [din(f"w_up{l}", [NHB, 128, 1024]) for l in range(2)]
    w_down = [din(f"w_down{l}", [NHB, 128, 1024]) for l in range(2)]
    pp_d = din("pp", [128, PP_COLS])
    pf_d = din("pf", [128, PF_COLS])
    wa_d = din("wa_bd", [4, 128, 128])
    wx_d = din("wx_bd", [4, 128, 128])
    cst_d = din("cst", [128, CST_COLS])
    dump_out = {}
    for nm, shp in dump.items():
        dump_out[nm] = nc.dram_tensor("dbg_" + nm, list(shp), F32, kind="ExternalOutput").ap()

    ARENA_BYTES = 206 * 1024
    arena = nc.alloc_sbuf_tensor("arena", [128, ARENA_BYTES // 2], BF16).ap()
    cur = [0]

    def alloc(name, dtype, cols):
        esz = 4 if dtype == F32 else 2
        nb = ((cols * esz + GR - 1) // GR) * GR
        off = cur[0]
        cur[0] += nb
        assert cur[0] <= ARENA_BYTES, (name, cur[0])
        return Buf(arena, off, dtype, cols, name)

    X = alloc("X", F32, 8 * S)
    XN = alloc("XN", BF16, 8 * S)
    Y = alloc("Y", BF16, 8 * S)
    NW = 8
    WR = [alloc(f"W{i}", BF16, 1024) for i in range(NW)]
    CSTB = alloc("CST", BF16, CST_COLS)
    PPB = alloc("PP", F32, PP_COLS + 60)
    PFB = alloc("PF", F32, PF_COLS)
    BD = alloc("BD", BF16, 8 * 128)
    RG = alloc("RG", BF16, 4 * 128)
    scratch0 = cur[0]
    SCR_BYTES = ARENA_BYTES - scratch0

    def scr_alloc_reset():
        cur[0] = scratch0

    Xv = X.ap.rearrange("p (c t) -> p c t", c=8)
    XNv = XN.ap.rearrange("p (c t) -> p c t", c=8)
    Yv = Y.ap.rearrange("p (c t) -> p c t", c=8)
    cst = CSTB.ap
    Ctab = cst[:, C_OFF:C_OFF + S]
    Stab = cst[:, S_OFF:S_OFF + S]
    ones_bf = cst[:, ONES_OFF:ONES_OFF + 128]
    bones_bf = cst[:, BONES_OFF:BONES_OFF + 128]
    ident_bf = cst[:, IDENT_OFF:IDENT_OFF + 128]
    pp = PPB.ap
    pf = PFB.ap
    DV = PP_COLS
    C_EPS, C_NSP, C_NSP2, C_NLAM, C_TMP, C_SUBG = DV, DV + 1, DV + 5, DV + 9, DV + 10, DV + 12

    ps_all = nc.alloc_psum_tensor("ps", [128, 8, 512], F32).ap()

    def PS(b):
        return ps_all[:, b, :]

    def PSR(b):
        return [f"ps{b}"]

    def xres(c, tt):
        return X.res(c * S + tt * TT, c * S + (tt + 1) * TT)

    def xnres(c, tt):
        return XN.res(c * S + tt * TT, c * S + (tt + 1) * TT)

    def yres(c, lo, hi):
        return Y.res(c * S + lo, c * S + hi)

    for i in range(0, CST_COLS, 1024):
        j = min(i + 1024, CST_COLS)
        P.op("pool", lambda e, i=i, j=j: e.dma_start(out=cst[:, i:j], in_=cst_d[:, i:j]),
             writes=CSTB.res(i, j), lane="cst")
    P.op("sp", lambda e: e.dma_start(out=pp[:, 0:PP_COLS], in_=pp_d), writes=PPB.res(), lane="io0")
    P.op("sp", lambda e: e.dma_start(out=pf, in_=pf_d), writes=PFB.res(), lane="io1")
    for c4 in range(4):
        P.op("pool", lambda e, c4=c4: e.dma_start(out=BD.ap[:, c4 * 128:(c4 + 1) * 128], in_=wa_d[c4]),
             writes=BD.res(), lane="cst")
        P.op("pool", lambda e, c4=c4: e.dma_start(out=BD.ap[:, (4 + c4) * 128:(5 + c4) * 128], in_=wx_d[c4]),
             writes=BD.res(), lane="cst")
    P.op("dve", lambda e: e.memset(pp[:, C_EPS:C_EPS + 1], EPS), writes=PPB.res())
    P.op("act", lambda e: e.activation(out=pp[:, C_NSP:C_NSP + 4], in_=pp[:, PP_LAM:PP_LAM + 4], func=AF.Exp, scale=-1.0),
         reads=PPB.res(), writes=PPB.res())
    P.op("act", lambda e: e.activation(out=pp[:, C_NSP:C_NSP + 4], in_=pp[:, C_NSP:C_NSP + 4], func=AF.Ln, bias=1.0),
         reads=PPB.res(), writes=PPB.res())
    P.op("dve", lambda e: e.tensor_scalar(out=pp[:, C_NSP2:C_NSP2 + 4], in0=pp[:, C_NSP:C_NSP + 4], scalar1=-16.0, scalar2=None, op0=ALU.mult),
         reads=PPB.res(), writes=PPB.res())
    P.op("dve", lambda e: e.tensor_scalar(out=pp[:, C_NSP:C_NSP + 4], in0=pp[:, C_NSP:C_NSP + 4], scalar1=-8.0, scalar2=None, op0=ALU.mult),
         reads=PPB.res(), writes=PPB.res())
    LAMBDA_INIT0 = 0.8 - 0.6 * math.exp(-0.3 * 0)
    for i in range(2):
        a0 = 128 + 128 * i
        P.op("dve", lambda e, a0=a0: e.tensor_tensor(out=pf[:, a0:a0 + 64], in0=pf[:, a0:a0 + 64], in1=pf[:, a0 + 64:a0 + 128], op=ALU.mult),
             reads=PFB.res(), writes=PFB.res())
        P.op("dve", lambda e, a0=a0, i=i: e.tensor_reduce(out=pp[:, C_TMP + i:C_TMP + i + 1], in_=pf[:, a0:a0 + 64], axis=AX.X, op=ALU.add),
             reads=PFB.res(), writes=PPB.res())
    P.op("act", lambda e: e.activation(out=pp[:, C_TMP:C_TMP + 2], in_=pp[:, C_TMP:C_TMP + 2], func=AF.Exp),
         reads=PPB.res(), writes=PPB.res())
    P.op("dve", lambda e: e.tensor_tensor(out=pp[:, C_NLAM:C_NLAM + 1], in0=pp[:, C_TMP + 1:C_TMP + 2], in1=pp[:, C_TMP:C_TMP + 1], op=ALU.subtract),
         reads=PPB.res(), writes=PPB.res())
    P.op("dve", lambda e: e.tensor_scalar(out=pp[:, C_NLAM:C_NLAM + 1], in0=pp[:, C_NLAM:C_NLAM + 1], scalar1=-LAMBDA_INIT0, scalar2=None, op0=ALU.add),
         reads=PPB.res(), writes=PPB.res())
    P.op("dve", lambda e: e.tensor_scalar(out=pp[:, C_SUBG:C_SUBG + 1], in0=pp[:, PP_SUBG:PP_SUBG + 1], scalar1=1.0 - LAMBDA_INIT0, scalar2=None, op0=ALU.mult),
         reads=PPB.res(), writes=PPB.res())
    P.op("dve", lambda e: e.tensor_scalar(out=pf[:, 0:128], in0=pf[:, 0:128], scalar1=1.0 - LAMBDA_INIT0, scalar2=None, op0=ALU.mult),
         reads=PFB.res(), writes=PFB.res())
    for i in range(4):
        P.op("dve", lambda e, i=i: e.tensor_scalar(out=RG.ap[:, i * 128:(i + 1) * 128], in0=cst[:, R_OFF:R_OFF + 128],
                                                    scalar1=pp[:, PP_QKG + i:PP_QKG + i + 1], scalar2=None, op0=ALU.mult),
             reads=CSTB.res() + PPB.res(), writes=RG.res())

    wcnt = [0]

    def load_w(src):
        slot = wcnt[0] % NW
        wcnt[0] += 1
        P.op("pool", lambda e, slot=slot, src=src: e.dma_start(out=WR[slot].ap, in_=src),
             writes=WR[slot].res(), lane=f"w{slot}")
        return WR[slot]

    pcnt = [0]

    def next_bank(lo=0, n=8):
        b = lo + pcnt[0] % n
        pcnt[0] += 1
        return b

    dbg_names = []

    def dump_buf(name, ap, reads):
        if name in dump_out:
            n = ap.shape[1]
            for i in range(0, n, 1024):
                P.op("pool", lambda e, i=i: e.dma_start(out=dump_out[name][:, i:i + 1024], in_=ap[:, i:i + 1024]), reads=reads,
                     writes=["dbg_" + name + str(i)], lane="dbg")
                dbg_names.append("dbg_" + name + str(i))

    def rmsnorm(norm_idx, T0):
        SQ = [alloc(f"nsq{i}", BF16, TT) for i in range(2)]
        RS = [alloc(f"nrs{i}", F32, TT) for i in range(2)]
        for tt in range(NTT):
            b = next_bank()
            ts = slice(tt * TT, (tt + 1) * TT)
            for c in range(8):
                sq = SQ[c % 2]
                P.op("act", lambda e, sq=sq, c=c, ts=ts: e.activation(out=sq.ap, in_=Xv[:, c, ts], func=AF.Square),
                     reads=xres(c, tt), writes=sq.res())
                P.op("pe", lambda e, sq=sq, c=c, b=b: e.matmul(PS(b), lhsT=ones_bf, rhs=sq.ap, start=(c == 0), stop=(c == 7)),
                     reads=sq.res() + CSTB.res(ONES_OFF, ONES_OFF + 128), writes=PSR(b))
            rs = RS[tt % 2]
            P.op("act", lambda e, rs=rs, b=b: e.activation(out=rs.ap, in_=PS(b), func=AF.Ln, scale=1.0 / D, bias=pp[:, C_EPS:C_EPS + 1]),
                 reads=PSR(b) + PPB.res(), writes=rs.res())
            P.op("act", lambda e, rs=rs: e.activation(out=rs.ap, in_=rs.ap, func=AF.Exp, scale=-0.5),
                 reads=rs.res(), writes=rs.res())
            for c in range(8):
                gcol = PP_NORM + 8 * norm_idx + c
                P.op("dve", lambda e, rs=rs, c=c, ts=ts, gcol=gcol: e.scalar_tensor_tensor(
                    out=XNv[:, c, ts], in0=Xv[:, c, ts], scalar=pp[:, gcol:gcol + 1], in1=rs.ap, op0=ALU.mult, op1=ALU.mult),
                     reads=xres(c, tt) + rs.res() + PPB.res(), writes=xnres(c, tt))

    def proj_fm(wb, tt, b):
        ts = slice(tt * TT, (tt + 1) * TT)
        wv = wb.ap.rearrange("p (c n) -> p c n", c=8)

        def fn(e):
            ins = None
            for c in range(8):
                ins = e.matmul(PS(b), lhsT=wv[:, c, :], rhs=XNv[:, c, ts], start=(c == 0), stop=(c == 7))
            return ins
        P.op("pe", fn, reads=wb.res() + [r for c in range(8) for r in xnres(c, tt)], writes=PSR(b))

    def qk_post(b, tt, rg_idx, dst, T):
        ts = slice(tt * TT, (tt + 1) * TT)
        sq, qb, rs, t1, t2 = T
        gcol = PP_QKG + rg_idx
        P.op("act", lambda e: e.activation(out=sq.ap, in_=PS(b), func=AF.Square), reads=PSR(b), writes=sq.res())
        P.op("act", lambda e: e.activation(out=qb.ap, in_=PS(b), func=AF.Copy, scale=pp[:, gcol:gcol + 1]),
             reads=PSR(b) + PPB.res(), writes=qb.res())
        b1 = next_bank()
        P.op("pe", lambda e: e.matmul(PS(b1), lhsT=bones_bf, rhs=sq.ap, start=True, stop=True),
             reads=sq.res() + CSTB.res(BONES_OFF, BONES_OFF + 128), writes=PSR(b1))
        b2 = next_bank()
        P.op("pe", lambda e: e.matmul(PS(b2), lhsT=cst[:, R_OFF:R_OFF + 128], rhs=qb.ap, start=True, stop=True),
             reads=qb.res() + CSTB.res(R_OFF, R_OFF + 128), writes=PSR(b2))
        P.op("act", lambda e: e.activation(out=rs.ap, in_=PS(b1), func=AF.Ln, scale=1.0 / 64, bias=pp[:, C_EPS:C_EPS + 1]),
             reads=PSR(b1) + PPB.res(), writes=rs.res())
        P.op("act", lambda e: e.activation(out=rs.ap, in_=rs.ap, func=AF.Exp, scale=-0.5), reads=rs.res(), writes=rs.res())
        P.op("act", lambda e: e.activation(out=t2.ap, in_=PS(b2), func=AF.Copy), reads=PSR(b2), writes=t2.res())
        P.op("dve", lambda e: e.tensor_tensor(out=t1.ap, in0=qb.ap, in1=Ctab[:, ts], op=ALU.mult),
             reads=qb.res() + CSTB.res(C_OFF + tt * TT, C_OFF + (tt + 1) * TT), writes=t1.res())
        P.op("dve", lambda e: e.tensor_tensor(out=t2.ap, in0=t2.ap, in1=Stab[:, ts], op=ALU.mult),
             reads=t2.res() + CSTB.res(S_OFF + tt * TT, S_OFF + (tt + 1) * TT), writes=t2.res())
        P.op("dve", lambda e: e.tensor_tensor(out=t1.ap, in0=t1.ap, in1=t2.ap, op=ALU.add),
             reads=t1.res() + t2.res(), writes=t1.res())
        if isinstance(dst, tuple):
            for hf, dd in enumerate(dst):
                pr = slice(64 * hf, 64 * hf + 64)
                P.op("dve", lambda e, pr=pr, dd=dd: e.tensor_tensor(out=dd.ap[pr, ts], in0=t1.ap[pr, :], in1=rs.ap[pr, :], op=ALU.mult),
                     reads=t1.res() + rs.res(), writes=dd.res(tt * TT, (tt + 1) * TT))
        else:
            P.op("dve", lambda e: e.tensor_tensor(out=dst.ap[:, ts], in0=t1.ap, in1=rs.ap, op=ALU.mult),
                 reads=t1.res() + rs.res(), writes=dst.res(tt * TT, (tt + 1) * TT))

    def qk_post_pair(tt, wq, wk, rgq, rgk, dstq, dstk, Tq, Tk, mid_fn=None):
        ts = slice(tt * TT, (tt + 1) * TT)
        bq = next_bank()
        proj_fm(wq, tt, bq)
        bk = next_bank()
        proj_fm(wk, tt, bk)
        items = [(bq, rgq, dstq, Tq), (bk, rgk, dstk, Tk)]
        first = True
        for (b, rg, dst, T) in items:
            sq, qb, rs, t1, t2 = T
            gcol = PP_QKG + rg
            P.op("act", lambda e, sq=sq, b=b: e.activation(out=sq.ap, in_=PS(b), func=AF.Square), reads=PSR(b), writes=sq.res())
            P.op("act", lambda e, qb=qb, b=b, gcol=gcol: e.activation(out=qb.ap, in_=PS(b), func=AF.Copy, scale=pp[:, gcol:gcol + 1]),
                 reads=PSR(b) + PPB.res(), writes=qb.res())
        if mid_fn is not None:
            mid_fn()
        bb = []
        for (b, rg, dst, T) in items:
            sq, qb, rs, t1, t2 = T
            b1 = next_bank()
            P.op("pe", lambda e, sq=sq, b1=b1: e.matmul(PS(b1), lhsT=bones_bf, rhs=sq.ap, start=True, stop=True),
                 reads=sq.res() + CSTB.res(BONES_OFF, BONES_OFF + 128), writes=PSR(b1))
            b2 = next_bank()
            P.op("pe", lambda e, qb=qb, b2=b2: e.matmul(PS(b2), lhsT=cst[:, R_OFF:R_OFF + 128], rhs=qb.ap, start=True, stop=True),
                 reads=qb.res() + CSTB.res(R_OFF, R_OFF + 128), writes=PSR(b2))
            bb.append((b1, b2))
        for (b, rg, dst, T), (b1, b2) in zip(items, bb):
            sq, qb, rs, t1, t2 = T
            P.op("act", lambda e, rs=rs, b1=b1: e.activation(out=rs.ap, in_=PS(b1), func=AF.Ln, scale=1.0 / 64, bias=pp[:, C_EPS:C_EPS + 1]),
                 reads=PSR(b1) + PPB.res(), writes=rs.res())
            P.op("act", lambda e, rs=rs: e.activation(out=rs.ap, in_=rs.ap, func=AF.Exp, scale=-0.5), reads=rs.res(), writes=rs.res())
            P.op("act", lambda e, t2=t2, b2=b2: e.activation(out=t2.ap, in_=PS(b2), func=AF.Copy), reads=PSR(b2), writes=t2.res())
        for (b, rg, dst, T) in items:
            sq, qb, rs, t1, t2 = T
            P.op("dve", lambda e, t1=t1, qb=qb: e.tensor_tensor(out=t1.ap, in0=qb.ap, in1=Ctab[:, ts], op=ALU.mult),
                 reads=qb.res() + CSTB.res(C_OFF + tt * TT, C_OFF + (tt + 1) * TT), writes=t1.res())
            P.op("dve", lambda e, t2=t2: e.tensor_tensor(out=t2.ap, in0=t2.ap, in1=Stab[:, ts], op=ALU.mult),
                 reads=t2.res() + CSTB.res(S_OFF + tt * TT, S_OFF + (tt + 1) * TT), writes=t2.res())
            P.op("dve", lambda e, t1=t1, t2=t2: e.tensor_tensor(out=t1.ap, in0=t1.ap, in1=t2.ap, op=ALU.add),
                 reads=t1.res() + t2.res(), writes=t1.res())
            if isinstance(dst, tuple):
                for hf, dd in enumerate(dst):
                    pr = slice(64 * hf, 64 * hf + 64)
                    P.op("dve", lambda e, pr=pr, dd=dd, t1=t1, rs=rs: e.tensor_tensor(out=dd.ap[pr, ts], in0=t1.ap[pr, :], in1=rs.ap[pr, :], op=ALU.mult),
                         reads=t1.res() + rs.res(), writes=dd.res(tt * TT, (tt + 1) * TT))
            else:
                P.op("dve", lambda e, dst=dst, t1=t1, rs=rs: e.tensor_tensor(out=dst.ap[:, ts], in0=t1.ap, in1=rs.ap, op=ALU.mult),
                     reads=t1.res() + rs.res(), writes=dst.res(tt * TT, (tt + 1) * TT))

    def v_proj(wb, VA, vw, col0, ncols, segs, groups=None):
        wv = wb.ap.rearrange("p (c n) -> p c n", c=8)
        VAv = VA.ap.rearrange("p (k w) -> p k w", w=vw)
        for g4 in (range(NQB // 4) if groups is None else groups):
            b = next_bank()

            def fn(e, g4=g4, b=b):
                ins = None
                for i in range(4):
                    blk = g4 * 4 + i
                    for c in range(8):
                        ins = e.matmul(PS(b)[:, i * 128:(i + 1) * 128], lhsT=XNv[:, c, blk * 128:(blk + 1) * 128], rhs=wv[:, c, :],
                                       start=(c == 0), stop=(c == 7))
                return ins
            P.op("pe", fn, reads=wb.res() + [r for c in range(8) for r in xnres(c, g4)], writes=PSR(b))
            psv = PS(b).rearrange("p (i n) -> p i n", i=4)
            for (slo, shi, dlo) in segs:
                P.op("act", lambda e, g4=g4, psv=psv, slo=slo, shi=shi, dlo=dlo: e.activation(
                    out=VAv[:, g4 * 4:(g4 + 1) * 4, dlo:dlo + (shi - slo)], in_=psv[:, :, slo:shi], func=AF.Copy),
                     reads=PSR(b), writes=VA.res(g4 * 4 * vw, (g4 + 1) * 4 * vw))

    def attention(QT, KT, VA, vw, dv, ET, units, mask_fn):
        VAv = VA.ap.rearrange("p (k w) -> p k w", w=vw)
        sbanks = [0, 1, 2]
        obanks = [4, 5, 6, 7]
        stages = []
        for (Q, prow, vcol, fin) in units:
            for j in range(4 * Q + 4):
                stages.append((Q, prow, vcol, fin, j))

        def stage_a(n):
            Q, prow, vcol, fin, j = stages[n]
            r = max(0, j - 4 * Q)
            lo = r * 128
            sb = sbanks[n % 3]
            et = ET[n % len(ET)]
            QZ = prow
            lm = mask_fn(et, Q, j, r)

            def fn(e):
                ins = e.matmul(PS(sb)[:, lo:TT], lhsT=KT.ap[:, j * 128:(j + 1) * 128],
                               rhs=QZ.ap[:, Q * TT + lo:(Q + 1) * TT], start=True, stop=(lm is None))
                if lm is not None:
                    plo, ncol, tlo = lm
                    ins = e.matmul(PS(sb)[:, plo:plo + ncol], lhsT=ident_bf, rhs=cst[:, tlo:tlo + ncol], start=False, stop=True)
                return ins
            P.op("pe", fn, reads=KT.res(j * 128, (j + 1) * 128) + QZ.res(Q * TT + lo, (Q + 1) * TT) + CSTB.res(ML1_OFF, IDENT_OFF + 128),
                 writes=PSR(sb))
            P.op("act", lambda e: e.activation(out=et.ap[:, lo:TT], in_=PS(sb)[:, lo:TT], func=AF.Exp, scale=0.125),
                 reads=PSR(sb), writes=et.res())

        def stage_b(n):
            Q, prow, vcol, fin, j = stages[n]
            r = max(0, j - 4 * Q)
            et = ET[n % len(ET)]
            for il in range(r, 4):
                i = 4 * Q + il
                ob = obanks[il]
                P.op("pe", lambda e, il=il, i=i, ob=ob: e.matmul(
                    PS(ob)[:, 0:dv + 1], lhsT=et.ap[:, il * 128:(il + 1) * 128], rhs=VAv[:, j, vcol:vcol + dv + 1],
                    start=(j == 0), stop=(j == i)),
                     reads=et.res() + VA.res(j * vw, (j + 1) * vw), writes=PSR(ob))
                if j == i:
                    fin(i, ob)

        LOOK = 4
        for n in range(min(LOOK, len(stages))):
            stage_a(n)
        for n in range(len(stages)):
            stage_b(n)
            if n + LOOK < len(stages):
                stage_a(n + LOOK)

    def stop(tag):
        if stop_after == tag:
            raise _Stop()

    def run_seq(s):
        for c in range(8):
            P.op("sp", lambda e, c=c, s=s: e.dma_start(out=Xv[:, c, :], in_=xT[s, c * 128:(c + 1) * 128, :]),
                 writes=X.res(c * S, (c + 1) * S), lane=f"io{c % 2}")

        scr_alloc_reset()
        rmsnorm(0, None)
        if s == 0:
            dump_buf("xn0", XN.ap, XN.res())
        scr_alloc_reset()
        XR = alloc("XR", F32, S + 4)
        U = alloc("U", F32, S)
        RB = alloc("RB", F32, S)
        IB = alloc("IB", F32, S)
        GG = alloc("GG", BF16, S)
        UB = alloc("UB", BF16, S)
        P.op("dve", lambda e: e.memset(XR.ap[:, 0:4], 0.0), writes=XR.res(0, 4))
        for c4 in range(4):
            wxr = load_w(w_in[c4])
            wgt = load_w(w_in[4 + c4])
            for tt in range(NTT):
                ts = slice(tt * TT, (tt + 1) * TT)
                b = next_bank()
                proj_fm(wxr, tt, b)
                P.op("act", lambda e, b=b, tt=tt: e.activation(out=XR.ap[:, 4 + tt * TT:4 + (tt + 1) * TT], in_=PS(b), func=AF.Copy),
                     reads=PSR(b), writes=XR.res(4 + tt * TT, 4 + (tt + 1) * TT))
                b = next_bank()
                proj_fm(wgt, tt, b)
                P.op("act", lambda e, b=b, ts=ts: e.activation(out=GG.ap[:, ts], in_=PS(b), func=AF.Gelu_apprx_tanh),
                     reads=PSR(b), writes=GG.res(tt * TT, (tt + 1) * TT))
            cw = PP_CONVW + 4 * c4
            P.op("dve", lambda e, cw=cw, c4=c4: e.tensor_scalar(out=U.ap, in0=XR.ap[:, 4:4 + S], scalar1=pp[:, cw + 3:cw + 4],
                                                                 scalar2=pp[:, PP_CONVB + c4:PP_CONVB + c4 + 1], op0=ALU.mult, op1=ALU.add),
                 reads=XR.res() + PPB.res(), writes=U.res())
            for j in range(3):
                P.op("dve", lambda e, cw=cw, j=j: e.scalar_tensor_tensor(out=U.ap, in0=XR.ap[:, 1 + j:1 + j + S], scalar=pp[:, cw + j:cw + j + 1],
                                                                         in1=U.ap, op0=ALU.mult, op1=ALU.add),
                     reads=XR.res() + U.res() + PPB.res(), writes=U.res())
            P.op("dve", lambda e: e.tensor_copy(out=UB.ap, in_=U.ap), reads=U.res(), writes=UB.res())
            for tt in range(NTT):
                ts = slice(tt * TT, (tt + 1) * TT)
                for (gi, dstb, bcol) in ((0, RB, PP_BA), (1, IB, PP_BX)):
                    b = next_bank()
                    P.op("pe", lambda e, b=b, gi=gi, c4=c4, ts=ts: e.matmul(PS(b), lhsT=BD.ap[:, (4 * gi + c4) * 128:(4 * gi + c4 + 1) * 128],
                                                                             rhs=UB.ap[:, ts], start=True, stop=True),
                         reads=BD.res() + UB.res(tt * TT, (tt + 1) * TT), writes=PSR(b))
                    P.op("act", lambda e, b=b, dstb=dstb, ts=ts, bcol=bcol, c4=c4: e.activation(
                        out=dstb.ap[:, ts], in_=PS(b), func=AF.Sigmoid, bias=pp[:, bcol + c4:bcol + c4 + 1]),
                         reads=PSR(b) + PPB.res(), writes=dstb.res(tt * TT, (tt + 1) * TT))
            Tap = XR.ap[:, 4:4 + S]
            P.op("act", lambda e, c4=c4: e.activation(out=Tap, in_=RB.ap, func=AF.Exp, scale=pp[:, C_NSP2 + c4:C_NSP2 + c4 + 1]),
                 reads=RB.res() + PPB.res(), writes=XR.res())
            P.op("act", lambda e, c4=c4: e.activation(out=RB.ap, in_=RB.ap, func=AF.Exp, scale=pp[:, C_NSP + c4:C_NSP + c4 + 1]),
                 reads=RB.res() + PPB.res(), writes=RB.res())
            P.op("act", lambda e: e.activation(out=Tap, in_=Tap, func=AF.Sqrt, scale=-1.0, bias=1.0), reads=XR.res(), writes=XR.res())
            P.op("dve", lambda e: e.tensor_tensor(out=IB.ap, in0=IB.ap, in1=U.ap, op=ALU.mult), reads=IB.res() + U.res(), writes=IB.res())
            P.op("dve", lambda e: e.tensor_tensor(out=IB.ap, in0=IB.ap, in1=Tap, op=ALU.mult), reads=IB.res() + XR.res(), writes=IB.res())
            P.op("dve", lambda e: e.tensor_tensor_scan(out=U.ap, data0=RB.ap, data1=IB.ap, initial=0.0, op0=ALU.mult, op1=ALU.add),
                 reads=RB.res() + IB.res(), writes=U.res())
            P.op("dve", lambda e, c4=c4: e.tensor_tensor(out=Yv[:, c4, :], in0=U.ap, in1=GG.ap, op=ALU.mult),
                 reads=U.res() + GG.res(), writes=yres(c4, 0, S))
        if s == 0:
            dump_buf("y0", Y.ap, Y.res())
        stop("rnn")
        scr_alloc_reset()
        QT = alloc("QT", BF16, S)
        QZ1 = alloc("QZ1", BF16, S)
        KT = alloc("KT", BF16, S)
        VW0 = 132
        VA = alloc("VA", BF16, NQB * VW0)
        ET = [alloc(f"ET{i}", BF16, TT) for i in range(5)]
        P.op("dve", lambda e: e.memset(QT.ap[64:128, :], 0.0), writes=QT.res())
        P.op("dve", lambda e: e.memset(QZ1.ap[0:64, :], 0.0), writes=QZ1.res())
        TQ = (alloc("tsq", BF16, TT), alloc("tqb", BF16, TT), alloc("trs", F32, TT), alloc("tt1", F32, TT), alloc("tt2", F32, TT))
        TQb = (alloc("tsqb", BF16, TT), alloc("tqbb", BF16, TT), alloc("trsb", F32, TT), alloc("tt1b", F32, TT), alloc("tt2b", F32, TT))
        OST = alloc("OST", F32, 132)
        OS = alloc("OS", BF16, 4 * 128)
        SM = alloc("SM", F32, 8)
        OAS = alloc("OAS", F32, 4 * 128)
        VAv0 = VA.ap.rearrange("p (k w) -> p k w", w=VW0)
        nxt0 = (load_w(w_in[8]), load_w(w_in[12]), load_w(w_in[16]))
        for h in range(4):
            wq, wk, wvv = nxt0
            if h < 3:
                nxt0 = (load_w(w_in[8 + h + 1]), load_w(w_in[12 + h + 1]), load_w(w_in[16 + h + 1]))
            for tt in range(NTT):
                qk_post_pair(tt, wq, wk, 0, 1, (QT, QZ1), KT, TQ, TQb,
                             mid_fn=lambda tt=tt, wvv=wvv: v_proj(wvv, VA, VW0, 0, 128, [(0, 128, 0)], groups=[tt]))
            if s == 0 and h == 0:
                dump_buf("qt0", QT.ap, QT.res())
                dump_buf("kt0", KT.ap, KT.res())
            stop("qk")
            P.op("dve", lambda e: e.memset(VAv0[:, :, 128:129], 1.0), writes=VA.res())
            stop("vproj")

            def mask0(et, Q, j, r):
                if j >= 4 * Q:
                    return (r * 128, 128, MCUR_OFF)
                return None

            def fin_b(i, ob, h=h):
                il = i % 4
                P.op("act", lambda e, ob=ob: e.activation(out=OST.ap[:, 0:129], in_=PS(ob)[:, 0:129], func=AF.Copy), reads=PSR(ob), writes=OST.res())
                P.op("dve", lambda e, ob=ob: e.reciprocal(out=SM.ap[:, 1:2], in_=OST.ap[:, 128:129]), reads=OST.res(), writes=SM.res())
                P.op("dve", lambda e: e.tensor_tensor(out=SM.ap[:, 1:2], in0=SM.ap[:, 1:2], in1=pp[:, C_NLAM:C_NLAM + 1], op=ALU.mult),
                     reads=SM.res() + PPB.res(), writes=SM.res())
                P.op("dve", lambda e, ob=ob, i=i, il=il: e.scalar_tensor_tensor(out=OS.ap[:, il * 128:(il + 1) * 128], in0=OST.ap[:, 0:128], scalar=SM.ap[:, 1:2],
                                                                                in1=OAS.ap[:, il * 128:(il + 1) * 128], op0=ALU.mult, op1=ALU.add),
                     reads=OST.res() + SM.res() + OAS.res(il * 128, (il + 1) * 128), writes=OS.res())
                if il == 3:
                    Q = i // 4
                    tb = 3
                    psb = PS(tb).bitcast(BF16)
                    sq, qb, rs, t1, t2 = TQ

                    def fn(e):
                        ins = None
                        for k in range(4):
                            ins = e.transpose(psb[:, k * 128:(k + 1) * 128], OS.ap[:, k * 128:(k + 1) * 128], ident_bf)
                        return ins
                    P.op("pe", fn, reads=OS.res() + CSTB.res(IDENT_OFF, IDENT_OFF + 128), writes=PSR(tb))
                    P.op("act", lambda e: e.activation(out=t1.ap, in_=psb[:, 0:TT], func=AF.Copy), reads=PSR(tb), writes=t1.res())
                    P.op("act", lambda e: e.activation(out=sq.ap, in_=psb[:, 0:TT], func=AF.Square), reads=PSR(tb), writes=sq.res())
                    P.op("pe", lambda e: e.matmul(PS(tb), lhsT=ones_bf, rhs=sq.ap, start=True, stop=True),
                         reads=sq.res() + CSTB.res(ONES_OFF, ONES_OFF + 128), writes=PSR(tb))
                    P.op("act", lambda e: e.activation(out=rs.ap, in_=PS(tb), func=AF.Ln, scale=1.0 / 128, bias=pp[:, C_EPS:C_EPS + 1]),
                         reads=PSR(tb) + PPB.res(), writes=rs.res())
                    P.op("act", lambda e: e.activation(out=rs.ap, in_=rs.ap, func=AF.Exp, scale=-0.5), reads=rs.res(), writes=rs.res())
                    P.op("dve", lambda e, Q=Q, h=h: e.scalar_tensor_tensor(out=Yv[:, 4 + h, Q * TT:(Q + 1) * TT], in0=t1.ap, scalar=pp[:, C_SUBG:C_SUBG + 1],
                                                                            in1=rs.ap, op0=ALU.mult, op1=ALU.mult),
                         reads=t1.res() + rs.res() + PPB.res(), writes=yres(4 + h, Q * TT, (Q + 1) * TT))

            def fin_a2(i, ob):
                P.op("act", lambda e, ob=ob: e.activation(out=OST.ap[:, 0:129], in_=PS(ob)[:, 0:129], func=AF.Copy), reads=PSR(ob), writes=OST.res())
                P.op("dve", lambda e, ob=ob: e.reciprocal(out=SM.ap[:, 0:1], in_=OST.ap[:, 128:129]), reads=OST.res(), writes=SM.res())
                P.op("dve", lambda e, ob=ob, i=i: e.tensor_scalar(out=OAS.ap[:, (i % 4) * 128:(i % 4 + 1) * 128], in0=OST.ap[:, 0:128], scalar1=SM.ap[:, 0:1],
                                                                   scalar2=None, op0=ALU.mult),
                     reads=OST.res() + SM.res(), writes=OAS.res((i % 4) * 128, (i % 4 + 1) * 128))

            units = [(Q, (QT, QZ1)[c], 0, (fin_a2, fin_b)[c]) for Q in range(NTT) for c in range(2)]
            if stop_after == "att_a":
                attention(QT, KT, VA, VW0, 128, ET, units[:NTT], mask0)
                stop("att_a")
            attention(QT, KT, VA, VW0, 128, ET, units, mask0)
            stop("att_b")
        if s == 0:
            dump_buf("y0b", Y.ap, Y.res())
        stop("attn0")

        def out_proj(wdram):
            for oc in range(8):
                wb = load_w(wdram[oc])
                wv = wb.ap.rearrange("p (c n) -> p c n", c=8)
                for tt in range(NTT):
                    ts = slice(tt * TT, (tt + 1) * TT)
                    b = next_bank()

                    def fn(e, wv=wv, ts=ts, b=b):
                        ins = None
                        for c in range(8):
                            ins = e.matmul(PS(b), lhsT=wv[:, c, :], rhs=Yv[:, c, ts], start=(c == 0), stop=(c == 7))
                        return ins
                    P.op("pe", fn, reads=wb.res() + [r for c in range(8) for r in yres(c, tt * TT, (tt + 1) * TT)], writes=PSR(b))
                    ev = EV[evc[0] % 2]
                    evc[0] += 1
                    P.op("act", lambda e, b=b, ev=ev: e.activation(out=ev.ap, in_=PS(b), func=AF.Copy), reads=PSR(b), writes=ev.res())
                    P.op("dve", lambda e, oc=oc, ts=ts, ev=ev: e.tensor_tensor(out=Xv[:, oc, ts], in0=Xv[:, oc, ts], in1=ev.ap, op=ALU.add),
                         reads=ev.res() + xres(oc, tt), writes=xres(oc, tt))

        scr_alloc_reset()
        EV = [alloc(f"ev{i}", F32, TT) for i in range(2)]
        evc = [0]
        out_proj(w_out0)
        if s == 0:
            dump_buf("x1", X.ap, X.res())
        stop("mix0")

        def ffn(l):
            scr_alloc_reset()
            rmsnorm(1 + 2 * l, None)
            scr_alloc_reset()
            HQ = 4
            H = alloc("H", BF16, HQ * S)
            Hv = H.ap.rearrange("p (k t) -> p k t", k=HQ)
            SG = [alloc(f"sg{i}", F32, TT) for i in range(2)]
            SU = [alloc(f"su{i}", F32, TT) for i in range(2)]
            EVF = [alloc(f"evf{i}", F32, TT) for i in range(2)]
            groups = [list(range(g, min(g + HQ, NHB))) for g in range(0, NHB, HQ)]
            for grp in groups:
                for k, hb in enumerate(grp):
                    wg = load_w(w_gate[l][hb])
                    wu = load_w(w_up[l][hb])
                    for tt in range(NTT):
                        ts = slice(tt * TT, (tt + 1) * TT)
                        bg = next_bank()
                        proj_fm(wg, tt, bg)
                        bu = next_bank()
                        proj_fm(wu, tt, bu)
                        sg = SG[tt % 2]
                        P.op("act", lambda e, bg=bg, sg=sg: e.activation(out=sg.ap, in_=PS(bg), func=AF.Silu), reads=PSR(bg), writes=sg.res())
                        su = SU[tt % 2]
                        P.op("act", lambda e, bu=bu, su=su: e.activation(out=su.ap, in_=PS(bu), func=AF.Copy), reads=PSR(bu), writes=su.res())
                        P.op("dve", lambda e, su=su, sg=sg, k=k, ts=ts: e.tensor_tensor(out=Hv[:, k, ts], in0=sg.ap, in1=su.ap, op=ALU.mult),
                             reads=su.res() + sg.res(), writes=H.res(k * S + tt * TT, k * S + (tt + 1) * TT))
                wds = [load_w(w_down[l][hb]) for hb in grp]
                for oc in range(8):
                    for tt in range(NTT):
                        ts = slice(tt * TT, (tt + 1) * TT)
                        b = next_bank()

                        def fn(e, oc=oc, ts=ts, b=b, n=len(grp), wds=wds):
                            ins = None
                            for k in range(n):
                                ins = e.matmul(PS(b), lhsT=wds[k].ap[:, oc * 128:(oc + 1) * 128], rhs=Hv[:, k, ts], start=(k == 0), stop=(k == n - 1))
                            return ins
                        P.op("pe", fn, reads=[r for w in wds for r in w.res()] + [r for k in range(len(grp)) for r in H.res(k * S + tt * TT, k * S + (tt + 1) * TT)],
                             writes=PSR(b))
                        ev = EVF[(oc * NTT + tt) % 2]
                        P.op("act", lambda e, b=b, ev=ev: e.activation(out=ev.ap, in_=PS(b), func=AF.Copy), reads=PSR(b), writes=ev.res())
                        P.op("dve", lambda e, oc=oc, ts=ts, ev=ev: e.tensor_tensor(out=Xv[:, oc, ts], in0=Xv[:, oc, ts], in1=ev.ap, op=ALU.add),
                             reads=ev.res() + xres(oc, tt), writes=xres(oc, tt))

        ffn(0)
        if s == 0:
            dump_buf("x2", X.ap, X.res())
        stop("ffn0")

        scr_alloc_reset()
        rmsnorm(2, None)
        scr_alloc_reset()
        QT = alloc("QT1", BF16, S)
        QZ1 = alloc("QZ11", BF16, S)
        KT = alloc("KT1", BF16, S)
        VW1 = 132
        VA = alloc("VA1", BF16, NQB * VW1)
        ET = [alloc(f"ET1{i}", BF16, TT) for i in range(6)]
        P.op("dve", lambda e: e.memset(QT.ap[64:128, :], 0.0), writes=QT.res())
        P.op("dve", lambda e: e.memset(QZ1.ap[0:64, :], 0.0), writes=QZ1.res())
        TQ = (alloc("tsq1", BF16, TT), alloc("tqb1", BF16, TT), alloc("trs1", F32, TT), alloc("tt11", F32, TT), alloc("tt21", F32, TT))
        OS = alloc("OS1", BF16, 4 * 128)
        SM = alloc("SM1", F32, 8)
        OST = alloc("OST1", F32, 132)
        VAv1 = VA.ap.rearrange("p (k w) -> p k w", w=VW1)
        nxt1 = (load_w(w_qkv[0]), load_w(w_qkv[8]), load_w(w_qkv[16]))
        TQ2 = (alloc("tsq2", BF16, TT), alloc("tqb2", BF16, TT), alloc("trs2", F32, TT), alloc("tt12", F32, TT), alloc("tt22", F32, TT))
        for hp in range(8):
            wq, wk, wvv = nxt1
            if hp < 7:
                nxt1 = (load_w(w_qkv[hp + 1]), load_w(w_qkv[8 + hp + 1]), load_w(w_qkv[16 + hp + 1]))
            for tt in range(NTT):
                qk_post_pair(tt, wq, wk, 2, 3, (QT, QZ1), KT, TQ, TQ2,
                             mid_fn=lambda tt=tt, wvv=wvv: v_proj(wvv, VA, VW1, 0, 128, [(0, 64, 0), (64, 128, 65)], groups=[tt]))
            P.op("dve", lambda e: e.memset(VAv1[:, :, 64:65], 1.0), writes=VA.res())
            P.op("dve", lambda e: e.memset(VAv1[:, :, 129:130], 1.0), writes=VA.res())

            def mask1(et, Q, j, r):
                d0 = 4 * Q + r - j
                return (r * 128, (4 - r) * 128, ML1_OFF + d0 * 128)

            def make_fin(h2, hp=hp):
                def fin(i, ob):
                    il = i % 4
                    P.op("act", lambda e, ob=ob: e.activation(out=OST.ap[:, 0:65], in_=PS(ob)[:, 0:65], func=AF.Copy), reads=PSR(ob), writes=OST.res())
                    P.op("dve", lambda e, ob=ob: e.reciprocal(out=SM.ap[:, h2:h2 + 1], in_=OST.ap[:, 64:65]), reads=OST.res(), writes=SM.res())
                    P.op("dve", lambda e, ob=ob, il=il: e.tensor_scalar(out=OS.ap[:, il * 128 + h2 * 64:il * 128 + h2 * 64 + 64], in0=OST.ap[:, 0:64],
                                                                         scalar1=SM.ap[:, h2:h2 + 1], scalar2=None, op0=ALU.mult),
                         reads=OST.res() + SM.res(), writes=OS.res())
                    if il == 3 and h2 == 1:
                        Q = i // 4
                        tb = 3
                        psb = PS(tb).bitcast(BF16)

                        def fn(e):
                            ins = None
                            for k in range(4):
                                ins = e.transpose(psb[:, k * 128:(k + 1) * 128], OS.ap[:, k * 128:(k + 1) * 128], ident_bf)
                            return ins
                        P.op("pe", fn, reads=OS.res() + CSTB.res(IDENT_OFF, IDENT_OFF + 128), writes=PSR(tb))
                        P.op("act", lambda e, Q=Q: e.activation(out=Yv[:, hp, Q * TT:(Q + 1) * TT], in_=psb[:, 0:TT], func=AF.Copy),
                             reads=PSR(tb), writes=yres(hp, Q * TT, (Q + 1) * TT))
                return fin
            fins = [make_fin(0), make_fin(1)]
            units = [(Q, (QT, QZ1)[h2], 65 * h2, fins[h2]) for Q in range(NTT) for h2 in range(2)]
            attention(QT, KT, VA, VW1, 64, ET, units, mask1)
        if s == 0:
            dump_buf("y1", Y.ap, Y.res())
        stop("attn1")
        scr_alloc_reset()
        EV = [alloc(f"ev{i}", F32, TT) for i in range(2)]
        out_proj(w_out1)
        if s == 0:
            dump_buf("x3", X.ap, X.res())
        ffn(1)
        for c in range(8):
            P.op("sp", lambda e, c=c, s=s: e.dma_start(out=oT[s, c * 128:(c + 1) * 128, :], in_=Xv[:, c, :]),
                 reads=X.res(c * S, (c + 1) * S), writes=[f"out{s}_{c}"], lane=f"io{c % 2}")
    try:
        for s in range(nseq):
            run_seq(s)
    except _Stop:
        pass
    P.op("sp", lambda e: e.nop(), reads=[f"out{s}_{c}" for s in range(nseq) for c in range(8)] + dbg_names)
    P.emit()
    return nc, P


_CACHE = {}


def kernel(**inputs):
    x = np.asarray(inputs["x"], dtype=np.float32)
    sh = prep_shared(inputs)
    if "nc" not in _CACHE:
        _CACHE["nc"] = build_program(2)[0]
    nc = _CACHE["nc"]
    in_maps = []
    for c in range(NCORES):
        m = dict(sh)
        m["xT"] = np.ascontiguousarray(x[2 * c:2 * c + 2].transpose(0, 2, 1))
        in_maps.append(m)
    res = run_bass_kernel_spmd(nc, in_maps, core_ids=list(range(NCORES)))
    out = np.empty_like(x)
    for c in range(NCORES):
        out[2 * c:2 * c + 2] = np.asarray(res.results[c]["oT"]).transpose(0, 2, 1)
    return out
```
